# Optimizing a Trainium2 kernel written in Bass

```python
import math
import jax, jax.numpy as jnp
from jax import lax
import numpy as np

D_MODEL = 1024
BATCH = 8
SEQ = 4096
DEPTH = 1

EPS = 1e-6
Q_BLOCK = 128
H_A = 8
QK_NOPE = 128
QK_ROPE = 64
V_DIM = 128
Q_LORA = 256
KV_LORA = 128
ROPE_THETA = 10000.0
H_B = 16
KV_B = 4
GROUP = H_B // KV_B
HD_B = 64
WINDOW = 128
NUM_BUCKETS = 32
MAX_DISTANCE = 128
D_FF = 2816
CONV_W = 3

W_IN_SIZES = (Q_LORA, KV_LORA + QK_ROPE, H_B * HD_B, KV_B * HD_B, KV_B * HD_B, D_MODEL, D_MODEL)
W_IN_COLS = sum(W_IN_SIZES)
W_IN_SPLITS = tuple(int(s) for s in np.cumsum(W_IN_SIZES)[:-1])

kernel_name = "hybrid_mla_swa_gated_convffn_encoder"


def rms_norm(x, g):
    xf = x.astype(jnp.float32)
    y = xf * lax.rsqrt(jnp.mean(xf * xf, axis=-1, keepdims=True) + EPS)
    return (y * g.astype(jnp.float32)).astype(x.dtype)


def apply_rope(x, positions):
    half = QK_ROPE // 2
    inv_freq = ROPE_THETA ** (-jnp.arange(half, dtype=jnp.float32) / half)
    ang = positions.astype(jnp.float32)[:, None] * inv_freq[None, :]
    cos = jnp.cos(ang)[None, :, None, :]
    sin = jnp.sin(ang)[None, :, None, :]
    xf = x.astype(jnp.float32)
    x1, x2 = xf[..., :half], xf[..., half:]
    out = jnp.concatenate([x1 * cos - x2 * sin, x2 * cos + x1 * sin], axis=-1)
    return out.astype(x.dtype)


def t5_bucket(rel):
    nb = NUM_BUCKETS // 2
    max_exact = nb // 2
    base = (rel > 0).astype(jnp.int32) * nb
    n = jnp.abs(rel)
    nf = jnp.maximum(n, 1).astype(jnp.float32)
    large = max_exact + (jnp.log(nf / max_exact) / math.log(MAX_DISTANCE / max_exact)
                         * (nb - max_exact)).astype(jnp.int32)
    large = jnp.minimum(large, nb - 1)
    return base + jnp.where(n < max_exact, n, large)


def mla_branch(q_lat, kv_lat, positions, q_a_norm_g, w_q_b, kv_a_norm_g, w_kv_b):
    B, S, _ = q_lat.shape
    q = (rms_norm(q_lat, q_a_norm_g) @ w_q_b).reshape(B, S, H_A, QK_NOPE + QK_ROPE)
    q = jnp.concatenate([q[..., :QK_NOPE], apply_rope(q[..., QK_NOPE:], positions)], axis=-1)
    c_kv, k_rope = kv_lat[..., :KV_LORA], kv_lat[..., KV_LORA:]
    kv = (rms_norm(c_kv, kv_a_norm_g) @ w_kv_b).reshape(B, S, H_A, QK_NOPE + V_DIM)
    k_nope, v = kv[..., :QK_NOPE], kv[..., QK_NOPE:]
    k_rope = apply_rope(k_rope[:, :, None, :], positions)
    k = jnp.concatenate([k_nope, jnp.broadcast_to(k_rope, (B, S, H_A, QK_ROPE))], axis=-1)
    scale = 1.0 / math.sqrt(QK_NOPE + QK_ROPE)
    nblk = S // Q_BLOCK
    qb = q.reshape(B, nblk, Q_BLOCK, H_A, QK_NOPE + QK_ROPE).transpose(1, 0, 2, 3, 4)

    def attend(q_blk):
        s = jnp.einsum('bqhd,bkhd->bhqk', q_blk, k, preferred_element_type=jnp.float32) * scale
        p = jax.nn.softmax(s, axis=-1).astype(v.dtype)
        return jnp.einsum('bhqk,bkhd->bqhd', p, v)

    o = lax.map(attend, qb)
    return o.transpose(1, 0, 2, 3, 4).reshape(B, S, H_A * V_DIM)


def window_branch(q, k, v, rel_bias, sinks):
    B, S, _ = q.shape
    q = q.reshape(B, S, KV_B, GROUP, HD_B)
    k = k.reshape(B, S, KV_B, HD_B)
    v = v.reshape(B, S, KV_B, HD_B)
    pad = ((0, 0), (WINDOW, WINDOW), (0, 0), (0, 0))
    kp = jnp.pad(k, pad)
    vp = jnp.pad(v, pad)
    span = Q_BLOCK + 2 * WINDOW
    a = jnp.arange(Q_BLOCK, dtype=jnp.int32)[:, None]
    c = jnp.arange(span, dtype=jnp.int32)[None, :]
    rel = c - WINDOW - a
    in_band = jnp.abs(rel) <= WINDOW
    bias = rel_bias[t5_bucket(rel)].astype(jnp.float32)
    bias = bias.transpose(2, 0, 1).reshape(KV_B, GROUP, Q_BLOCK, span)
    sink = sinks.astype(jnp.float32).reshape(1, KV_B, GROUP, 1, 1)
    scale = 1.0 / math.sqrt(HD_B)
    nblk = S // Q_BLOCK
    qb = q.reshape(B, nblk, Q_BLOCK, KV_B, GROUP, HD_B).transpose(1, 0, 2, 3, 4, 5)

    def attend(args):
        q_blk, n = args
        start = n * Q_BLOCK
        k_blk = lax.dynamic_slice_in_dim(kp, start, span, axis=1)
        v_blk = lax.dynamic_slice_in_dim(vp, start, span, axis=1)
        key_pos = start - WINDOW + c
        valid = in_band & (key_pos >= 0) & (key_pos < S)
        s = jnp.einsum('bqhgd,bkhd->bhgqk', q_blk, k_blk,
                       preferred_element_type=jnp.float32) * scale + bias
        s = jnp.where(valid, s, -1e30)
        sink_col = jnp.broadcast_to(sink, (B, KV_B, GROUP, Q_BLOCK, 1))
        p = jax.nn.softmax(jnp.concatenate([s, sink_col], axis=-1), axis=-1)[..., :span]
        return jnp.einsum('bhgqk,bkhd->bqhgd', p.astype(v_blk.dtype), v_blk)

    o = lax.map(attend, (qb, jnp.arange(nblk, dtype=jnp.int32)))
    return o.transpose(1, 0, 2, 3, 4, 5).reshape(B, S, H_B * HD_B)


def conv_ffn(h, w_up, conv_w, conv_b, w_down):
    u = h @ w_up
    up = jnp.pad(u, ((0, 0), (1, 1), (0, 0)))
    u = up[:, :-2] * conv_w[0] + up[:, 1:-1] * conv_w[1] + up[:, 2:] * conv_w[2] + conv_b
    g, val = u[..., :D_FF], u[..., D_FF:]
    return (jax.nn.silu(g) * val) @ w_down


def setup_inputs(seed: int = 0) -> dict:
    key = jax.random.key(seed)
    ks = jax.random.split(key, 20)
    f32 = jnp.float32
    L = DEPTH

    def nrm(k, shape, scale):
        return jax.random.normal(k, shape, f32) * scale

    def gain(k, shape):
        return 1.0 + 0.05 * jax.random.normal(k, shape, f32)

    return {
        "x": jax.random.normal(ks[0], (BATCH, SEQ, D_MODEL), f32),
        "positions": jnp.arange(SEQ, dtype=jnp.int32),
        "norm1_g": gain(ks[1], (L, D_MODEL)),
        "w_in": nrm(ks[2], (L, D_MODEL, W_IN_COLS), D_MODEL ** -0.5),
        "q_a_norm_g": gain(ks[3], (L, Q_LORA)),
        "w_q_b": nrm(ks[4], (L, Q_LORA, H_A * (QK_NOPE + QK_ROPE)), Q_LORA ** -0.5),
        "kv_a_norm_g": gain(ks[5], (L, KV_LORA)),
        "w_kv_b": nrm(ks[6], (L, KV_LORA, H_A * (QK_NOPE + V_DIM)), KV_LORA ** -0.5),
        "rel_bias": nrm(ks[7], (NUM_BUCKETS, H_B), 0.5),
        "sinks": nrm(ks[8], (L, H_B), 0.5),
        "w_out": nrm(ks[9], (L, D_MODEL, D_MODEL), D_MODEL ** -0.5),
        "norm2_g": gain(ks[10], (L, D_MODEL)),
        "w_up": nrm(ks[11], (L, D_MODEL, 2 * D_FF), D_MODEL ** -0.5),
        "conv_w": nrm(ks[12], (L, CONV_W, 2 * D_FF), CONV_W ** -0.5),
        "conv_b": nrm(ks[13], (L, 2 * D_FF), 0.02),
        "w_down": nrm(ks[14], (L, D_FF, D_MODEL), D_FF ** -0.5),
        "final_norm_g": gain(ks[15], (D_MODEL,)),
    }


def reference(x, positions, norm1_g, w_in, q_a_norm_g, w_q_b, kv_a_norm_g, w_kv_b, rel_bias,
              sinks, w_out, norm2_g, w_up, conv_w, conv_b, w_down, final_norm_g):
    for l in range(DEPTH):
        h = rms_norm(x, norm1_g[l])
        proj = h @ w_in[l]
        q_lat, kv_lat, q_b, k_b, v_b, gate_a, gate_b = jnp.split(proj, W_IN_SPLITS, axis=-1)
        o_a = mla_branch(q_lat, kv_lat, positions, q_a_norm_g[l], w_q_b[l],
                         kv_a_norm_g[l], w_kv_b[l])
        o_b = window_branch(q_b, k_b, v_b, rel_bias, sinks[l])
        mixed = jax.nn.sigmoid(gate_a) * o_a + jax.nn.sigmoid(gate_b) * o_b
        x = x + mixed @ w_out[l]
        x = x + conv_ffn(rms_norm(x, norm2_g[l]), w_up[l], conv_w[l], conv_b[l], w_down[l])
    return rms_norm(x, final_norm_g)
```

```python
import math
from contextlib import ExitStack

import numpy as np
import concourse.bass as bass
import concourse.mybir as mybir
from concourse.bass_utils import run_bass_kernel_spmd

F32 = mybir.dt.float32
BF16 = mybir.dt.bfloat16
I32 = mybir.dt.int32
AF = mybir.ActivationFunctionType
ALU = mybir.AluOpType

D = 1024
NDK = 8
H_A = 8
H_B = 16
KV_B = 4
D_FF = 2816
NF = 22
W_IN_COLS = 4032
C_QLAT, C_CKV, C_KR, C_QB, C_KB, C_VB, C_GA, C_GB = 0, 256, 384, 448, 1472, 1728, 1984, 3008
EPS = 1e-6
NEG = -30000.0
SC_A = 1.0 / math.sqrt(192.0)
SC_B = 1.0 / math.sqrt(64.0)
ENG_NAMES = ("pe", "act", "dve", "pool", "sp")
SAME_ENGINE_INORDER = ()


class Sched:
    def __init__(self, nc, sems, dma_sems):
        self.nc = nc
        self.sem = sems
        self.cnt = {e: 0 for e in ENG_NAMES}
        self.waited = {e: {} for e in ENG_NAMES}
        self.prog = {e: [] for e in ENG_NAMES}
        self.last_w = {}
        self.readers = {}
        self.dma_sems = dma_sems
        self.dma_slot = {}
        self.out_tokens = []

    def _need(self, eng, tok, waits):
        if tok is None:
            return
        key, sem, val = tok
        if key == eng and eng in SAME_ENGINE_INORDER:
            return
        if self.waited[eng].get(key, 0) >= val:
            return
        cur = waits.get(key)
        if cur is None or cur[1] < val:
            waits[key] = (sem, val)

    def _deps(self, eng, reads, writes):
        waits = {}
        for r in reads:
            self._need(eng, self.last_w.get(r), waits)
        for w in writes:
            self._need(eng, self.last_w.get(w), waits)
            for t in self.readers.get(w, ()):
                self._need(eng, t, waits)
        for key, (sem, val) in waits.items():
            self.waited[eng][key] = val
        return list(waits.values())

    def _commit(self, tok, reads, writes):
        for r in reads:
            lst = self.readers.setdefault(r, [])
            lst[:] = [t for t in lst if t[0] != tok[0]]
            lst.append(tok)
        for w in writes:
            self.last_w[w] = tok
            self.readers[w] = []

    def op(self, eng, fn, reads=(), writes=()):
        waits = self._deps(eng, reads, writes)
        self.cnt[eng] += 1
        val = self.cnt[eng]
        sem = self.sem[eng]
        tok = (eng, sem, val)

        def emit(e, waits=waits, fn=fn, sem=sem):
            for (s, v) in waits:
                e.wait_ge(s, v)
            fn(e).then_inc(sem, 1)

        self.prog[eng].append(emit)
        self._commit(tok, reads, writes)
        return tok

    def dma(self, eng, slot, fns, reads=(), writes=(), is_output=False):
        if slot not in self.dma_slot:
            self.dma_slot[slot] = [self.dma_sems.pop(), 0]
        st = self.dma_slot[slot]
        sem = st[0]
        waits = self._deps(eng, reads, writes)
        st[1] += 16 * len(fns)
        tok = ("dma:" + slot, sem, st[1])

        def emit(e, waits=waits, fns=fns, sem=sem):
            for (s, v) in waits:
                e.wait_ge(s, v)
            for f in fns:
                f(e).then_inc(sem, 16)

        self.prog[eng].append(emit)
        self._commit(tok, reads, writes)
        if is_output:
            self.out_tokens.append(tok)
        return tok

    def fence(self):
        toks = [(e, self.sem[e], self.cnt[e]) for e in ENG_NAMES if self.cnt[e] > 0]
        toks += [("dma:" + k, v[0], v[1]) for k, v in self.dma_slot.items() if v[1] > 0]
        for eng in ENG_NAMES:
            waits = {}
            for t in toks:
                if t[0] == eng:
                    continue
                self._need(eng, t, waits)
            for key, (sem, val) in waits.items():
                self.waited[eng][key] = val
            wl = list(waits.values())

            def emit(e, wl=wl):
                for (s_, v) in wl:
                    e.wait_ge(s_, v)

            self.prog[eng].append(emit)
        self.last_w.clear()
        self.readers.clear()

    def final_wait(self, eng):
        last = {}
        for (k, s, v) in self.out_tokens:
            if k not in last or last[k][1] < v:
                last[k] = (s, v)

        def emit(e, last=last):
            for (s, v) in last.values():
                e.wait_ge(s, v)

        self.prog[eng].append(emit)

    def run(self, block):
        prog = self.prog

        @block.tensor
        def _(e):
            for f in prog["pe"]:
                f(e)

        @block.scalar
        def _(e):
            for f in prog["act"]:
                f(e)

        @block.vector
        def _(e):
            for f in prog["dve"]:
                f(e)

        @block.gpsimd
        def _(e):
            for f in prog["pool"]:
                f(e)

        @block.sync
        def _(e):
            for f in prog["sp"]:
                f(e)


def t5_bucket_np(rel):
    nb = 16
    max_exact = 8
    base = (rel > 0).astype(np.int64) * nb
    n = np.abs(rel)
    nf = np.maximum(n, 1).astype(np.float32)
    large = max_exact + (np.log(nf / np.float32(max_exact)) / np.float32(math.log(128 / max_exact))
                         * np.float32(nb - max_exact)).astype(np.int64)
    large = np.minimum(large, nb - 1)
    return base + np.where(n < max_exact, n, large)


def host_consts():
    c = {}
    c["c_ident"] = np.eye(128, dtype=np.float32)
    c["c_J"] = np.eye(128, dtype=np.float32)[::-1].copy()
    G = np.zeros((33, 512), dtype=np.float32)
    for s in range(512):
        rel = 255 - s
        if abs(rel) <= 128 and s <= 510:
            G[int(t5_bucket_np(np.array([rel]))[0]), s] = 1.0
        else:
            G[32, s] = 1.0
    c["c_G"] = G
    half = 32
    inv = (np.float32(10000.0) ** (-np.arange(half, dtype=np.float32) / np.float32(half))).astype(np.float32)
    rope = np.zeros((64, 2), dtype=np.float32)
    rope[:, 0] = np.concatenate([inv, inv])
    rope[:, 1] = np.concatenate([-np.ones(half), np.ones(half)])
    c["c_rope"] = rope
    return c


def build(NCH, dbg=()):
    S_LEN = 512 * NCH
    NT = 4 * NCH
    nc = bass.Bass("TRN2", target_bir_lowering=False)

    def din(name, shape, dt=F32):
        return nc.dram_tensor(name, list(shape), dt, kind="ExternalInput").ap()

    x_d = din("x", [S_LEN, D])
    pos_d = din("positions", [S_LEN], I32)
    g1_d = din("norm1_g", [1, D])
    win_d = din("w_in", [1, D, W_IN_COLS])
    gq_d = din("q_a_norm_g", [1, 256])
    wqb_d = din("w_q_b", [1, 256, 1536])
    gkv_d = din("kv_a_norm_g", [1, 128])
    wkvb_d = din("w_kv_b", [1, 128, 2048])
    rb_d = din("rel_bias", [32, 16])
    sinks_d = din("sinks", [1, 16])
    wout_d = din("w_out", [1, D, D])
    g2_d = din("norm2_g", [1, D])
    wup_d = din("w_up", [1, D, 2 * D_FF])
    cw_d = din("conv_w", [1, 3, 2 * D_FF])
    cb_d = din("conv_b", [1, 2 * D_FF])
    wdn_d = din("w_down", [1, D_FF, D])
    gf_d = din("final_norm_g", [D])
    cid_d = din("c_ident", [128, 128])
    cJ_d = din("c_J", [128, 128])
    cG_d = din("c_G", [33, 512])
    crope_d = din("c_rope", [64, 2])
    out_d = nc.dram_tensor("out", [S_LEN, D], F32, kind="ExternalOutput").ap()
    uscr = nc.dram_tensor("uscr", [16, 512], F32, kind="Internal")
    rscr = nc.dram_tensor("rscr", [NCH, 64, 1024], F32, kind="Internal")
    tscr = nc.dram_tensor("tscr", [4, 128, 3072], BF16, kind="Internal")
    dbg_out = {}
    for (nm, shp) in dbg:
        dbg_out[nm] = nc.dram_tensor("dbg_" + nm, list(shp), F32, kind="ExternalOutput").ap()

    win_v = win_d[0].rearrange("(dk p) n -> p dk n", p=128)
    wout_v = wout_d[0].rearrange("(dk p) n -> p dk n", p=128)
    wup_v = wup_d[0].rearrange("(dk p) n -> p dk n", p=128)
    wdn_v = wdn_d[0].rearrange("(f p) n -> p f n", p=128)

    with ExitStack() as es:
        def sb(name, shape, dt):
            return es.enter_context(nc.sbuf_tensor(name, list(shape), dt))

        def psum(name, shape, dt):
            return es.enter_context(nc.psum_tensor(name, list(shape), dt))

        ckvnT = sb("ckvnT", [128, S_LEN], BF16)
        ckvn = sb("ckvn", [128, NT, 128], BF16)
        krT = sb("krT", [128, S_LEN], BF16)
        kbT = sb("kbT", [128, 2, S_LEN], BF16)
        vb = sb("vb", [128, NT, 256], BF16)
        Wqabs = sb("Wqabs", [128, 2, 8, 128], BF16)
        Wqr = sb("Wqr", [128, 2, 8, 128], BF16)
        Wvb = sb("Wvb", [128, 8, 128], BF16)
        identb = sb("identb", [128, 128], BF16)
        onesb = sb("onesb", [128, 128], BF16)
        onesf = sb("onesf", [128, 128], F32)
        identf = sb("identf", [128, 128], F32)
        Jf = sb("Jf", [128, 128], F32)
        colsA = sb("colsA", [128, 64], F32)
        colsB = sb("colsB", [128, 132], F32)
        gf_b = sb("gf_b", [128, D], F32)
        sinkexp = sb("sinkexp", [128, 16], F32)
        ropec = sb("ropec", [64, 2], F32)
        epsc = sb("epsc", [128, 1], F32)
        halfpi = sb("halfpi", [128, 1], F32)
        ss = sb("ss", [128, 8], F32)
        HB = sb("HB", [128, NCH, 8, 2], BF16)
        xbuf = sb("xbuf", [128, 2, 4, D], F32)
        h2T = sb("h2T", [128, 2, 8, 512], BF16)
        ring = sb("ring", [128, 2, 4096], BF16)
        B16 = sb("B16", [128, 24, 512], BF16)
        TF = sb("TF", [128, 9, 512], F32)
        xn = sb("xn", [128, D], BF16)
        qlnT = sb("qlnT", [128, 2, 512], BF16)
        qabsT = sb("qabsT", [128, 2, 512], BF16)
        qrT = sb("qrT", [128, 2, 512], BF16)
        PT = sb("PT", [128, 2, 1024], BF16)
        Tbuf = sb("Tbuf", [128, 2, 3, 512], BF16)
        sk2 = sb("sk2", [128, 8], F32)
        xn2 = sb("xn2", [128, D], BF16)
        rtab = sb("rtab", [64, 2, 512], F32)

        rtmp = TF[0:64, 0:3, :]
        rtmi = sb("rtmi", [64, 512], I32)

        psS = [psum("psS0", [128, 1024], F32), psum("psS1", [128, 1024], F32)]
        psO = psum("psO", [128, 512], F32)
        psD = psum("psD", [128, 512], F32)
        psM = psum("psM", [128, 512], F32)
        psT = psum("psT", [128, 1024], BF16)
        psTf = psT[:].bitcast(F32)

        sems = {e: es.enter_context(nc.semaphore("s_" + e)) for e in ENG_NAMES}
        dsems = [es.enter_context(nc.semaphore("d%d" % i)) for i in range(34)]
        block = es.enter_context(nc.Block())
        S = Sched(nc, sems, dsems)

        def ACT(out, in_, func, reads, writes, **kw):
            return S.op("act", lambda e: e.activation(out=out, in_=in_, func=func, **kw), reads, writes)

        def TT(out, in0, in1, op, reads, writes, eng="dve"):
            return S.op(eng, lambda e: e.tensor_tensor(out=out, in0=in0, in1=in1, op=op), reads, writes)

        def TS(out, in0, s1, s2, op0, op1, reads, writes, eng="dve"):
            if s2 is None:
                return S.op(eng, lambda e: e.tensor_scalar(out=out, in0=in0, scalar1=s1, scalar2=None, op0=op0), reads, writes)
            return S.op(eng, lambda e: e.tensor_scalar(out=out, in0=in0, scalar1=s1, scalar2=s2, op0=op0, op1=op1), reads, writes)

        def STT(out, in0, scalar, in1, op0, op1, reads, writes, eng="dve"):
            return S.op(eng, lambda e: e.scalar_tensor_tensor(out=out, in0=in0, scalar=scalar, in1=in1, op0=op0, op1=op1), reads, writes)

        def CP(out, in_, reads, writes, eng="dve"):
            return S.op(eng, lambda e: e.tensor_copy(out=out, in_=in_), reads, writes)

        def RCP(out, in_, reads, writes):
            return S.op("dve", lambda e: e.reciprocal(out=out, in_=in_), reads, writes)

        def MSET(ap, val, writes, eng="pool"):
            return S.op(eng, lambda e: e.memset(ap, val), (), writes)

        def MMG(mms, reads, writes):
            def fn(e, mms=mms):
                ins = None
                for (o, l, r, st, sp) in mms:
                    ins = e.matmul(o, lhsT=l, rhs=r, start=st, stop=sp)
                return ins
            return S.op("pe", fn, reads, writes)

        def TRS(items, reads, writes):
            def fn(e, items=items):
                ins = None
                for (o, i) in items:
                    ins = e.transpose(o, i, identb[:])
                return ins
            return S.op("pe", fn, list(reads) + ["identb"], writes)

        def DMA(eng, slot, pairs, reads, writes, is_output=False):
            fns = [(lambda e, o=o, i=i: e.dma_start(out=o, in_=i)) for (o, i) in pairs]
            return S.dma(eng, slot, fns, reads, writes, is_output=is_output)

        def bc_last(ap, n):
            return bass.AP(ap.tensor, ap.offset, list(ap.ap) + [[0, n]])

        HT = ["b16_%d" % i for i in range(8)]
        MIX = ["b16_%d" % i for i in range(8, 16)]
        QW = ["b16_%d" % i for i in range(16, 24)]
        hT = B16[:, 0:8, :]

        def tfr(i):
            return "tf%d" % i

        MSET(epsc[:], EPS, ["epsc"])
        MSET(halfpi[:], math.pi / 2, ["halfpi"])
        MSET(onesb[:], 1.0, ["onesb"])
        MSET(onesf[:], 1.0, ["onesf"])
        MSET(HB[:], 0.0, ["HB%d" % c for c in range(NCH)])
        MSET(ss[:], 0.0, ["ss"])
        MSET(krT[64:128, :], 0.0, ["krT"])
        MSET(qrT[64:128, :, :], 0.0, ["qrT0", "qrT1"])
        DMA("sp", "c0", [(identf[:], cid_d[:, :]), (Jf[:], cJ_d[:, :]), (ropec[:], crope_d[:, :]),
                         (gf_b[:], gf_d.partition_broadcast(128)),
                         (sinkexp[:], sinks_d[0].partition_broadcast(128))],
            [], ["identf", "Jf", "ropec", "gf_b", "sinkexp"])
        CP(identb[:], identf[:], ["identf"], ["identb"])
        ACT(sinkexp[:], sinkexp[:], AF.Exp, ["sinkexp"], ["sinkexp"])

        if True:
            wqbf = xbuf[:, 0, 0:3, :].rearrange("p a n -> p (a n)").rearrange("p (a n) -> p a n", a=2)
            wkvf = xbuf[:, 1, 0:2, :].rearrange("p a n -> p (a n)")
            rowsA = TF[0:64, 0, 0:128]
            rowsB = TF[:, 0, 128:256]
            rowsC = TF[0:4, 0, 256:384]
            ATs = [(TF[:, 1, 0:256].rearrange("p (a b) -> p a b", b=128), "AT0"),
                   (TF[:, 2, 128:384].rearrange("p (a b) -> p a b", b=128), "AT1")]
            BTs = [(TF[:, 1, 256:384], "BT0"), (TF[:, 2, 384:512], "BT1")]
            rbx = TF[0:33, 2, 0:16]
            Gs = TF[0:33, 3, :]
            us = TF[0:16, 4, :]
            Tt = TF[:, 5:8, :].rearrange("p a b -> p (a b)").rearrange("p (a b c) -> p a b c", a=3, b=4)
            Hk = B16[:].rearrange("p a b -> p (a b)").bitcast(F32).rearrange("p (a b c) -> p a b c", a=3, b=16)

            sbk = [(psM, "psM"), (psO, "psO"), (psD, "psD"), (psS[0], "psS0a"), (psS[1], "psS1a")]
            srot = {"i": 0}

            def MMC(out_sl, lhsT, rhs, rreads, dst, dst_reads, dst_writes, rows=slice(0, 128), fn=None):
                bk, bn = sbk[srot["i"] % len(sbk)]
                srot["i"] += 1
                pv_ = bk[rows, out_sl]
                MMG([(pv_, lhsT, rhs, True, True)], rreads, [bn])
                if fn is None:
                    CP(dst, pv_, [bn] + list(dst_reads), dst_writes)
                else:
                    fn(pv_, bn)

            MSET(rowsA, 0.0, ["rowsA"])
            cwrows = cw_d[0].rearrange("t (f p) -> (t f) p", p=128)
            DMA("sp", "c1", [(rowsA[0:8, :], g1_d[0].rearrange("(a p) -> a p", p=128)),
                             (rowsA[8:16, :], g2_d[0].rearrange("(a p) -> a p", p=128)),
                             (rowsA[16:18, :], gq_d[0].rearrange("(a p) -> a p", p=128)),
                             (rowsA[18:19, :], gkv_d[0].rearrange("(a p) -> a p", p=128)),
                             (rowsA[19:63, :], cb_d[0].rearrange("(a p) -> a p", p=128)),
                             (rowsB, cwrows[0:128, :]),
                             (rowsC, cwrows[128:132, :])],
                ["rowsA"], ["rowsA", "rowsB", "rowsC"])
            MMC(slice(0, 64), rowsA, identf[0:64, 0:64], ["rowsA", "identf"], colsA[:], [], ["colsA"])
            MMC(slice(0, 128), rowsB, identf[:, :], ["rowsB", "identf"], colsB[:, 0:128], [], ["colsB"])
            MMC(slice(0, 4), rowsC, identf[0:4, 0:4], ["rowsC", "identf"], colsB[:, 128:132], ["colsB"], ["colsB"])
            g1T = colsA[:, 0:8]
            g2T = colsA[:, 8:16]
            gqT = colsA[:, 16:18]
            gkvT = colsA[:, 18:19]

            MSET(rbx, NEG, ["rbx"])
            DMA("sp", "c3", [(rbx[0:32, :], rb_d[:, :]), (Gs, cG_d[:, :])], ["rbx"], ["rbx", "Gs"])
            MMG([(psM[0:16, :], rbx, Gs, True, True)], ["rbx", "Gs"], ["psM"])
            CP(us, psM[0:16, :], ["psM"], ["us"])
            DMA("sp", "c4", [(uscr.ap()[:, :], us)], ["us"], ["uscr"])
            base_tt = (256, 128, 0)
            pairs = []
            for tt in range(3):
                for h in range(16):
                    gg, jj = divmod(h, 4)
                    hpos = 4 * gg + (jj % 2) * 2 + jj // 2
                    pairs.append((Hk[:, tt, hpos, :], bass.AP(uscr, h * 512 + base_tt[tt], [[1, 128], [1, 128]])))
            DMA("sp", "c5", pairs, ["uscr"], ["Hk"])
            DMA("sp", "c2", [(wqbf, wqb_d[0].rearrange("(a p) n -> p a n", p=128)),
                             (wkvf, wkvb_d[0])], [], ["wqbf", "wkvf"])
            for a in range(2):
                TS(wqbf[:, a, :], wqbf[:, a, :], gqT[:, a:a + 1], None, ALU.mult, None, ["wqbf", "colsA"], ["wqbf"])
            TS(wkvf, wkvf, gkvT, None, ALU.mult, None, ["wkvf", "colsA"], ["wkvf"])
            for h in range(H_A):
                for a in range(2):
                    b0 = h * 192 + 128
                    CP(Wqr[:, a, h, 0:64], wqbf[:, a, b0:b0 + 64], ["wqbf"], ["Wqr"])
                    CP(Wqr[:, a, h, 64:96], wqbf[:, a, b0 + 32:b0 + 64], ["wqbf", "Wqr"], ["Wqr"])
                    CP(Wqr[:, a, h, 96:128], wqbf[:, a, b0:b0 + 32], ["wqbf", "Wqr"], ["Wqr"])
                CP(Wvb[:, h, :], wkvf[:, h * 256 + 128:h * 256 + 256], ["wkvf"], ["Wvb"], eng="pool")
                ATh = ATs[h % 2]
                BTh = BTs[h % 2]
                for a in range(2):
                    MMC(slice(0, 128), wqbf[:, a, h * 192:h * 192 + 128], identf[:, :], ["wqbf", "identf"],
                        ATh[0][:, a, :], [], [ATh[1]])
                MMC(slice(0, 128), wkvf[:, h * 256:h * 256 + 128], identf[:, :], ["wkvf", "identf"], BTh[0], [], [BTh[1]])
                for a in range(2):
                    MMC(slice(0, 128), ATh[0][:, a, :], BTh[0], [ATh[1], BTh[1]], Wqabs[:, a, h, :], [], ["Wqabs"])

            Thl = ring[:, 0, 0:3072].rearrange("p (a b c) -> p a b c", a=2, b=3)
            tmpT = TF[:, 8, :]
            for g in range(4):
                for tt in range(3):
                    MMG([(psM[:, :], Jf[:, :], Hk[:, tt, 4 * g:4 * g + 4, :].rearrange("p a b -> p (a b)"), True, True)],
                        ["Jf", "Hk"], ["psM"])
                    t8 = Tt[:, tt, :, :].rearrange("p a b -> p (a b)")
                    TS(t8, psM[:, :], 1.0 / SC_B, None, ALU.mult, None, ["psM"], ["Tt"])
                    CP(Thl[:, 0, tt, :], t8, ["Tt"], ["Thl"])
                    CP(tmpT, Thl[:, 0, tt, :], ["Thl"], ["tmpT"])
                    TT(tmpT, t8, tmpT, ALU.subtract, ["Tt", "tmpT"], ["tmpT"])
                    CP(Thl[:, 1, tt, :], tmpT, ["tmpT", "Thl"], ["Thl"])
                DMA("sp", "c6", [(tscr.ap()[g], ring[:, 0, 0:3072])], ["Thl"], ["tscr"])
            CP(sk2[0:64, :], sinkexp[0:64, 0:16:2], ["sinkexp"], ["sk2"])
            CP(sk2[64:128, :], sinkexp[64:128, 1:16:2], ["sinkexp", "sk2"], ["sk2"])

        def rms_rstd(src, nfeat, col, src_reads, junk, junk_names):
            c = ss[:, col:col + 1]
            nm = "ss%d" % col
            ACT(junk, src, AF.Square, list(src_reads) + [nm], list(junk_names) + [nm], accum_out=c)
            ACT(c, c, AF.Ln, [nm, "epsc"], [nm], scale=1.0 / nfeat, bias=epsc[:, 0:1])
            ACT(c, c, AF.Exp, [nm], [nm], scale=-0.5)
            return c, nm

        psMb = psM[:].bitcast(BF16)
        xns = [(xn, "xn"), (xn2, "xn2")]
        trb = [(psT[:, :], "psT"), (psMb, "psM")]

        def norm_group(items, gT, gname):
            n = len(items)
            cols = {}

            def S_(k):
                xtile, xreads, dst, dst_names, col0, pre = items[k]
                if pre is not None:
                    pre()
                cols[k] = rms_rstd(xtile, D, 0 if k % 2 == 0 else 3, xreads,
                                   TF[:, 7:9, :].rearrange("p a b -> p (a b)"), [tfr(7), tfr(8)])

            def X_(k):
                xtile, xreads, dst, dst_names, col0, pre = items[k]
                c, nm = cols[k]
                xb_, xbn = xns[k % 2]
                TS(xb_[:], xtile, c, None, ALU.mult, None, list(xreads) + [nm], [xbn])

            def T_(k):
                xb_, xbn = xns[k % 2]
                tb, tbn = trb[k % 2]
                TRS([(tb[:, dk * 128:(dk + 1) * 128], xb_[:, dk * 128:(dk + 1) * 128]) for dk in range(8)], [xbn], [tbn])

            def E_(k):
                xtile, xreads, dst, dst_names, col0, pre = items[k]
                tb, tbn = trb[k % 2]
                TT(dst[:, :, col0:col0 + 128], tb.rearrange("p (a b) -> p a b", b=128), bc_last(gT, 128), ALU.mult,
                   [tbn, gname], dst_names)

            for k in range(n):
                S_(k)
                X_(k)
                T_(k)
                if k >= 1:
                    E_(k - 1)
            E_(n - 1)

        def rope_tables(c):
            DMA("sp", "pos", [(rtmi[:], pos_d[c * 512:(c + 1) * 512].partition_broadcast(64))], [], ["rtmi"])
            a, n, t = rtmp[:, 0, :], rtmp[:, 1, :], rtmp[:, 2, :]
            CP(a, rtmi[:], ["rtmi"], ["tf0"])
            TS(a, a, ropec[:, 0:1], None, ALU.mult, None, ["tf0", "ropec"], ["tf0"])
            TS(n, a, float(1.0 / (2 * math.pi)), None, ALU.mult, None, ["tf0"], ["tf1"])
            CP(rtmi[:], n, ["tf1"], ["rtmi"])
            CP(n, rtmi[:], ["rtmi"], ["tf1"])
            STT(a, n, -6.28125, a, ALU.mult, ALU.add, ["tf1", "tf0"], ["tf0"])
            STT(a, n, -float(2 * math.pi - 6.28125), a, ALU.mult, ALU.add, ["tf1", "tf0"], ["tf0"])
            TS(n, a, float(math.pi), float(-2 * math.pi), ALU.is_gt, ALU.mult, ["tf0"], ["tf1"])
            TT(a, a, n, ALU.add, ["tf0", "tf1"], ["tf0"])
            TS(n, a, float(-math.pi), float(2 * math.pi), ALU.is_lt, ALU.mult, ["tf0"], ["tf1"])
            TT(a, a, n, ALU.add, ["tf0", "tf1"], ["tf0"])
            ACT(t, a, AF.Sin, ["tf0"], ["tf2"])
            TS(rtab[:, 1, :], t, ropec[:, 1:2], None, ALU.mult, None, ["tf2", "ropec"], ["rtab1"])
            STT(n, a, -1.0, a, ALU.mult, ALU.max, ["tf0"], ["tf1"])
            ACT(rtab[:, 0, :], n, AF.Sin, ["tf1", "halfpi"], ["rtab0"], scale=-1.0, bias=halfpi[0:64, :])

        def rope_apply(psA, psA_names, psB, psB_names, dst, dst_names):
            t0, t1 = rtmp[:, 0, :], rtmp[:, 1, :]
            TT(t0, psA, rtab[:, 0, :], ALU.mult, list(psA_names) + ["rtab0"], ["tf0"])
            TT(t1, psB, rtab[:, 1, :], ALU.mult, list(psB_names) + ["rtab1"], ["tf1"])
            TT(dst, t0, t1, ALU.add, ["tf0", "tf1"], dst_names)

        def gen_pieces():
            seq = []

            def m_pieces():
                seq.extend([("win", C_QLAT, 256), ("win", C_QB, 512), ("win", C_GB, 512),
                            ("win", C_QB + 512, 512), ("win", C_GB + 512, 512),
                            ("win", C_GA, 512), ("win", C_GA + 512, 512),
                            ("wout", 0, 512), ("wout", 512, 512)])

            def f_pieces():
                for hf in range(2):
                    for i in range(6):
                        seq.append(("wup", hf * 11 + 2 * i, 1 if i == 5 else 2))
                    for i in range(3):
                        seq.append(("wdown", hf * 11 + 4 * i, 3 if i == 2 else 4))

            m_pieces()
            for c in range(NCH):
                if c + 1 < NCH:
                    m_pieces()
                f_pieces()
            return seq

        pieces = gen_pieces()
        wstate = {"next_issue": 0, "next_use": 0}

        def w_issue():
            i = wstate["next_issue"]
            if i >= len(pieces):
                return
            wstate["next_issue"] = i + 1
            slot = i % 2
            kind, a, b = pieces[i]
            rg = ring[:, slot, :]
            nm = "ring%d" % slot
            if kind == "win":
                pr = [(rg[:, 0:8 * b].rearrange("p (k n) -> p k n", n=b), win_v[:, :, a:a + b])]
            elif kind == "wout":
                pr = [(rg[:, 0:8 * b].rearrange("p (k n) -> p k n", n=b), wout_v[:, :, a:a + b])]
            elif kind == "wup":
                v = rg.rearrange("p (k n) -> p k n", n=512)
                pr = [(v[:, :, 0:128 * b], wup_v[:, :, a * 128:(a + b) * 128]),
                      (v[:, :, 256:256 + 128 * b], wup_v[:, :, D_FF + a * 128:D_FF + (a + b) * 128])]
            else:
                v = rg.rearrange("p (f n) -> p f n", n=1024)
                pr = [(v[:, 0:b, :], wdn_v[:, a:a + b, :])]
            DMA("pool", nm, pr, [], [nm])

        def w_next(kind):
            i = wstate["next_use"]
            wstate["next_use"] = i + 1
            assert pieces[i][0] == kind, (pieces[i], kind)
            while wstate["next_issue"] <= i:
                w_issue()
            return ring[:, i % 2, :], "ring%d" % (i % 2), pieces[i]

        def w_prefetch():
            if wstate["next_issue"] <= wstate["next_use"]:
                w_issue()

        S.fence()
        MSET(B16[:, 16:24, :], 0.0, QW, eng="dve")
        if True:
            Wp1 = ring[:].rearrange("p a b -> p (a b)")[:, 0:6144].rearrange("p (k n) -> p k n", n=768)
            DMA("pool", "wp1", [(Wp1[:, :, 0:128], win_v[:, :, C_CKV:C_CKV + 128]),
                                (Wp1[:, :, 128:384], win_v[:, :, C_VB:C_VB + 256]),
                                (Wp1[:, :, 384:448], win_v[:, :, C_KR:C_KR + 64]),
                                (Wp1[:, :, 448:480], win_v[:, :, C_KR + 32:C_KR + 64]),
                                (Wp1[:, :, 480:512], win_v[:, :, C_KR:C_KR + 32]),
                                (Wp1[:, :, 512:768], win_v[:, :, C_KB:C_KB + 256])], [], ["Wp1"])
            hTs = [(B16[:, 0:8, :], HT), (B16[:, 8:16, :], MIX)]

            def p1_norms(c):
                hTc, HTc = hTs[c % 2]
                items = []
                for j in range(4):
                    t = 4 * c + j
                    xt = xbuf[:, c % 2, j, :]
                    xnm = "x%d_%d" % (c % 2, j)
                    DMA("sp", "xin" + xnm, [(xt, x_d[t * 128:(t + 1) * 128, :])], [], [xnm])
                    items.append((xt, [xnm], hTc, HTc, j * 128, None))
                norm_group(items, g1T, "colsA")

            def p1_proj(c):
                hT, HT = hTs[c % 2]
                MMG([(psO[:, :], Wp1[:, dk, 384:512], hT[:, dk, :], dk == 0, dk == 7) for dk in range(8)],
                    HT + ["Wp1"], ["psO"])
                rope_apply(psO[0:64, :], ["psO"], psO[64:128, :], ["psO"], krT[0:64, c * 512:(c + 1) * 512], ["krT"])
                for m in range(2):
                    bank = psS[m][:, 0:512]
                    bn = "psS%da" % m
                    MMG([(bank, Wp1[:, dk, 512 + m * 128:512 + (m + 1) * 128], hT[:, dk, :], dk == 0, dk == 7)
                         for dk in range(8)], HT + ["Wp1"], [bn])
                    ACT(kbT[:, m, c * 512:(c + 1) * 512], bank, AF.Copy, [bn], ["kbT"])
                for j in range(4):
                    t = 4 * c + j
                    par = j % 2
                    bk = psS[par][:, 512:1024]
                    bn = "psS%db" % par
                    MMG([(bk[:, 0:384], hT[:, dk, j * 128:(j + 1) * 128], Wp1[:, dk, 0:384], dk == 0, dk == 7)
                         for dk in range(8)], HT + ["Wp1"], [bn])
                    cc, nm = rms_rstd(bk[:, 0:128], 128, 1 if par == 0 else 4, [bn], TF[:, 8, 0:128], [tfr(8)])
                    TS(ckvn[:, t, :], bk[:, 0:128], cc, None, ALU.mult, None, [bn, nm], ["ckvn"])
                    ACT(vb[:, t, :], bk[:, 128:384], AF.Copy, [bn], ["vb"])
                    tb, tbn = trb[par]
                    TRS([(tb[:, 0:128], ckvn[:, t, :])], ["ckvn"], [tbn])
                    CP(ckvnT[:, t * 128:(t + 1) * 128], tb[:, 0:128], [tbn], ["ckvnT"])

            p1_norms(0)
            rope_tables(0)
            for c in range(NCH):
                if c + 1 < NCH:
                    p1_norms(c + 1)
                p1_proj(c)
                if c + 1 < NCH:
                    rope_tables(c + 1)
        S.fence()

        qw_v = B16[:, 16:24, :].rearrange("p a b -> p (a b)").rearrange("p (g t j q) -> p g t j q", g=2, t=4, j=4)
        mixT = B16[:, 8:16, :]

        def sigmoid_den(dst, src_ps, src_names, dst_name):
            ACT(dst, src_ps, AF.Exp, src_names, [dst_name], scale=-1.0)
            ACT(dst, dst, AF.Ln, [dst_name], [dst_name], bias=1.0)
            ACT(dst, dst, AF.Exp, [dst_name], [dst_name], scale=-1.0)

        def recip_act(dst, src, src_names, dst_names):
            ACT(dst, src, AF.Ln, src_names, dst_names)
            ACT(dst, dst, AF.Exp, dst_names, dst_names, scale=-1.0)

        preloaded = set()

        def stage_M(c):
            xb_i = c % 2
            rope_tables(c)
            items = []
            for j in range(4):
                t = 4 * c + j
                xt = xbuf[:, xb_i, j, :]
                xnm = "x%d_%d" % (xb_i, j)
                if (c, j) not in preloaded:
                    DMA("sp", "xin" + xnm, [(xt, x_d[t * 128:(t + 1) * 128, :])], [], [xnm])
                items.append((xt, [xnm], hT, HT, j * 128, None))
            norm_group(items, g1T, "colsA")
            rg, rn, _ = w_next("win")
            w_prefetch()
            wq = rg[:, 0:8 * 256].rearrange("p (k n) -> p k n", n=256)
            qcols = {}

            def qMM(j):
                bk, bn = (psO, "psO") if j % 2 == 0 else (psD, "psD")
                MMG([(bk[:, 0:256], hT[:, dk, j * 128:(j + 1) * 128], wq[:, dk, :], dk == 0, dk == 7) for dk in range(8)],
                    HT + [rn], [bn])

            def qSX(j):
                bk, bn = (psO, "psO") if j % 2 == 0 else (psD, "psD")
                cc, nm = rms_rstd(bk[:, 0:256], 256, 1 if j % 2 == 0 else 4, [bn], TF[:, 8, 0:256], [tfr(8)])
                xb_, xbn = xns[j % 2]
                TS(xb_[:, 0:256], bk[:, 0:256], cc, None, ALU.mult, None, [bn, nm], [xbn])

            def qT(j):
                xb_, xbn = xns[j % 2]
                tb, tbn = trb[j % 2]
                TRS([(tb[:, a * 128:(a + 1) * 128], xb_[:, a * 128:(a + 1) * 128]) for a in range(2)], [xbn], [tbn])

            def qE(j):
                tb, tbn = trb[j % 2]
                CP(qlnT[:, :, j * 128:(j + 1) * 128], tb[:, 0:256].rearrange("p (a b) -> p a b", b=128), [tbn], ["qlnT"])

            qMM(0)
            for j in range(4):
                if j + 1 < 4:
                    qMM(j + 1)
                qSX(j)
                qT(j)
                if j >= 1:
                    qE(j - 1)
            qE(3)

            wqb = [None, None]
            rg, rn, _ = w_next("win")
            wqb[0] = (rg[:, 0:4096].rearrange("p (k n) -> p k n", n=512), rn)
            w_prefetch()
            gB = [None, None]

            rot = {"q": 0, "s": 0}
            qbanks = [(psM[:, :], "psM"), (psTf[:, :], "psT"), (psO[:, :], "psO"), (psD[:, :], "psD")]
            sbanks = [(psS[0][:, 0:512], "psS0a"), (psS[1][:, 0:512], "psS1a"),
                      (psS[0][:, 512:1024], "psS0b"), (psS[1][:, 512:1024], "psS1b")]

            def win_pass(gp):
                wv, rn = wqb[gp]
                for gl in range(2):
                    g = 2 * gp + gl
                    half = g % 2
                    for jp in range(2):
                        hq = g * 4 + 2 * jp
                        colb = (hq * 64) % 512
                        bk, bn = qbanks[rot["q"] % 4]
                        rot["q"] += 1
                        MMG([(bk, wv[:, dk, colb:colb + 128], hT[:, dk, :], dk == 0, dk == 7) for dk in range(8)],
                            HT + [rn], [bn])
                        for e in range(2):
                            jj = 2 * jp + e
                            jpos = (jj % 2) * 2 + jj // 2
                            src = bk[e * 64:e * 64 + 64, :]
                            ACT(qw_v[half * 64:half * 64 + 64, gl, :, jpos, :], src.rearrange("p (t q) -> p t q", q=128),
                                AF.Copy, [bn], QW)

            def gate_sig(wv, rn, fcl, tfi):
                bk, bn = qbanks[rot["q"] % 2]
                rot["q"] += 1
                MMG([(bk, wv[:, dk, fcl * 128:(fcl + 1) * 128], hT[:, dk, :], dk == 0, dk == 7) for dk in range(8)],
                    HT + [rn], [bn])
                sigmoid_den(TF[:, tfi, :], bk, [bn], tfr(tfi))
                return (TF[:, tfi, :], tfr(tfi))

            ptsets = [[(PT[:, 0, 0:512], "PT0a"), (PT[:, 0, 512:1024], "PT0b"),
                       (TF[:, 0, :].bitcast(BF16)[:, 0:512], tfr(0))],
                      [(PT[:, 1, 0:512], "PT1a"), (PT[:, 1, 512:1024], "PT1b"),
                       (TF[:, 1, :].bitcast(BF16)[:, 0:512], tfr(1))]]
            odsets = [((psO, "psO"), (psD, "psD")), ((psM, "psM"), (psTf, "psT"))]

            def win_attend_pass(gp):
                its = []
                for gl in range(2):
                    g = 2 * gp + gl
                    for t4 in range(4):
                        its.append((g, gl, t4))
                state = {"g": None}

                def front(n, g, gl, t4):
                    if state["g"] != g:
                        state["g"] = g
                        for k in range(2):
                            gate_sig(gB[gp][0], gB[gp][1], 2 * gl + k, 5 + k)
                        DMA("sp", "tld", [(Tbuf[:].rearrange("p a b c -> p (a b c)"), tscr.ap()[g])], ["tscr"], ["Tbuf"])
                    t = 4 * c + t4
                    kts = [kt for kt in (t - 1, t, t + 1) if 0 <= kt < NT]
                    rhs_q = qw_v[:, gl, t4, :, :].rearrange("p j q -> p (j q)")
                    for idx, kt in enumerate(kts):
                        tt = kt - t + 1
                        bk, bn = sbanks[rot["s"] % 4]
                        rot["s"] += 1
                        MMG([(bk, kbT[:, g // 2, kt * 128:(kt + 1) * 128], rhs_q, True, False),
                             (bk, identb[:, :], Tbuf[:, 0, tt, :], False, False),
                             (bk, identb[:, :], Tbuf[:, 1, tt, :], False, True)],
                            ["kbT", "identb", "Tbuf"] + QW, [bn])
                        pt, pn = ptsets[n % 2][idx]
                        ACT(pt, bk, AF.Exp, [bn], [pn], scale=SC_B)
                    return kts

                def back(n, g, gl, t4, kts):
                    (Ob, On), (Db, Dn) = odsets[n % 2]
                    mmo, mmd, rd = [], [], []
                    L = len(kts)
                    for idx, kt in enumerate(kts):
                        pt, pn = ptsets[n % 2][idx]
                        rd.append(pn)
                        vv = vb[:, kt, g * 64:(g + 1) * 64]
                        mmo.append((Ob[0:64, 0:256], vv, pt[:, 0:256], idx == 0, idx == L - 1))
                        mmd.append((Db[0:64, 0:256], onesb[:, 0:64], pt[:, 0:256], idx == 0, idx == L - 1))
                    for idx, kt in enumerate(kts):
                        pt, pn = ptsets[n % 2][idx]
                        vv = vb[:, kt, g * 64:(g + 1) * 64]
                        mmo.append((Ob[64:128, 0:256], vv, pt[:, 256:512], idx == 0, idx == L - 1))
                        mmd.append((Db[64:128, 0:256], onesb[:, 0:64], pt[:, 256:512], idx == 0, idx == L - 1))
                    MMG(mmo, rd + ["vb"], [On])
                    MMG(mmd, rd + ["onesb"], [Dn])
                    den = TF[:, 3, 0:256]
                    TT(den.rearrange("p (k q) -> p k q", q=128), Db[:, 0:256].rearrange("p (k q) -> p k q", q=128),
                       bc_last(sk2[:, 2 * g:2 * g + 2], 128), ALU.add, [Dn, "sk2"], [tfr(3)])
                    recip_act(den, den, [tfr(3)], [tfr(3)])
                    ob = TF[:, 4, 0:256]
                    TT(ob, Ob[:, 0:256], den, ALU.mult, [On, tfr(3)], [tfr(4)])
                    TT(mixT[:, 2 * g:2 * g + 2, t4 * 128:(t4 + 1) * 128], ob.rearrange("p (k q) -> p k q", q=128),
                       TF[:, 5:7, t4 * 128:(t4 + 1) * 128], ALU.mult, [tfr(4), tfr(5), tfr(6)], [MIX[2 * g], MIX[2 * g + 1]])

                prev = None
                for n, (g, gl, t4) in enumerate(its):
                    if prev is not None and prev[1] != g:
                        back(*prev)
                        prev = None
                    kts = front(n, g, gl, t4)
                    if prev is not None:
                        back(*prev)
                    prev = (n, g, gl, t4, kts)
                back(*prev)

            win_pass(0)
            rg, rn, _ = w_next("win")
            gB[0] = (rg[:, 0:4096].rearrange("p (k n) -> p k n", n=512), rn)
            w_prefetch()
            win_attend_pass(0)
            rg, rn, _ = w_next("win")
            wqb[1] = (rg[:, 0:4096].rearrange("p (k n) -> p k n", n=512), rn)
            w_prefetch()
            win_pass(1)
            rg, rn, _ = w_next("win")
            gB[1] = (rg[:, 0:4096].rearrange("p (k n) -> p k n", n=512), rn)
            w_prefetch()
            win_attend_pass(1)

            def q_project(h, qb):
                MMG([(psM[:, :], Wqabs[:, a, h, :], qlnT[:, a, :], a == 0, a == 1) for a in range(2)],
                    ["Wqabs", "qlnT"], ["psM"])
                ACT(qabsT[:, qb, :], psM[:, :], AF.Copy, ["psM"], ["qabsT%d" % qb])
                MMG([(psTf[:, :], Wqr[:, a, h, :], qlnT[:, a, :], a == 0, a == 1) for a in range(2)],
                    ["Wqr", "qlnT"], ["psT"])
                rope_apply(psTf[0:64, :], ["psT"], psTf[64:128, :], ["psT"], qrT[0:64, qb, :], ["qrT%d" % qb])

            gA = [None, None]
            pend = {"fin": None}
            heads = []
            q_project(0, 0)
            NP = NT // 2
            for h in range(H_A):
                qb = h % 2
                def qk(p, h=h, qb=qb):
                    for u in range(2):
                        kt = 2 * p + u
                        bank = psS[p % 2][:, u * 512:(u + 1) * 512]
                        MMG([(bank, ckvnT[:, kt * 128:(kt + 1) * 128], qabsT[:, qb, :], True, False),
                             (bank, krT[:, kt * 128:(kt + 1) * 128], qrT[:, qb, :], False, True)],
                            ["ckvnT", "krT", "qabsT%d" % qb, "qrT%d" % qb], ["psS%d%s" % (p % 2, "ab"[u])])
                    ACT(PT[:, p % 2, :], psS[p % 2][:, :], AF.Exp, ["psS%da" % (p % 2), "psS%db" % (p % 2)],
                        ["PT%da" % (p % 2), "PT%db" % (p % 2)], scale=SC_A)

                def pv(p):
                    mmo, mmd = [], []
                    for u in range(2):
                        kt = 2 * p + u
                        pt = PT[:, p % 2, u * 512:(u + 1) * 512]
                        mmo.append((psO[:, :], ckvn[:, kt, :], pt, kt == 0, kt == NT - 1))
                        mmd.append((psD[:, :], onesb[:, :], pt, kt == 0, kt == NT - 1))
                    MMG(mmo + mmd, ["PT%da" % (p % 2), "PT%db" % (p % 2), "ckvn", "onesb"], ["psO", "psD"])

                def fin_a(h=h):
                    ACT(TF[:, 3, :], psD[:, :], AF.Ln, ["psD"], [tfr(3)])
                    CP(TF[:, 4, :], psO[:, :], ["psO"], [tfr(4)])

                def fin_b(h=h):
                    if h % 4 == 0:
                        rg, rn, _ = w_next("win")
                        gA[h // 4] = (rg[:, 0:4096].rearrange("p (k n) -> p k n", n=512), rn)
                        w_prefetch()
                    rec = TF[:, 3, :]
                    ACT(rec, rec, AF.Exp, [tfr(3)], [tfr(3)], scale=-1.0)
                    ol = TF[:, 2, :].bitcast(BF16)[:, 0:512]
                    TT(ol, TF[:, 4, :], rec, ALU.mult, [tfr(4), tfr(3)], [tfr(2)])
                    MMG([(psM[:, :], Wvb[:, h, :], ol, True, True)], ["Wvb", tfr(2)], ["psM"])
                    oa = TF[:, 4, :]
                    CP(oa, psM[:, :], ["psM"], [tfr(4)])
                    wv, rn = gA[h // 4]
                    sg = gate_sig(wv, rn, h % 4, 5)
                    TT(oa, oa, sg[0], ALU.mult, [tfr(4), sg[1]], [tfr(4)])
                    TT(mixT[:, h, :], mixT[:, h, :], oa, ALU.add, [MIX[h], tfr(4)], [MIX[h]])

                heads.append((qk, pv, fin_a, fin_b))

            prev = None
            for h in range(H_A):
                for p in range(NP):
                    heads[h][0](p)
                    if prev is not None:
                        ph, pp = prev
                        heads[ph][1](pp)
                        if pp == NP - 1:
                            heads[ph][2]()
                            pend["fin"] = heads[ph][3]
                    prev = (h, p)
                    if p == min(2, NP - 1) and h + 1 < H_A:
                        q_project(h + 1, (h + 1) % 2)
                    if p == min(3, NP - 1) and pend["fin"] is not None:
                        pend["fin"]()
                        pend["fin"] = None
            heads[H_A - 1][1](NP - 1)
            heads[H_A - 1][2]()
            if pend["fin"] is not None:
                pend["fin"]()
            heads[H_A - 1][3]()
            pend["fin"] = None

            wo = []
            for i in range(2):
                rg, rn, _ = w_next("wout")
                wo.append((rg[:, 0:4096].rearrange("p (k n) -> p k n", n=512), rn))
            for j in range(4):
                xt = xbuf[:, xb_i, j, :]
                xnm = "x%d_%d" % (xb_i, j)
                for hf in range(2):
                    bank = psS[hf][:, 0:512]
                    bn = "psS%da" % hf
                    MMG([(bank, mixT[:, dk, j * 128:(j + 1) * 128], wo[hf][0][:, dk, :], dk == 0, dk == 7) for dk in range(8)],
                        MIX + [wo[hf][1]], [bn])
                    TT(xt[:, hf * 512:(hf + 1) * 512], xt[:, hf * 512:(hf + 1) * 512], bank, ALU.add, [xnm, bn], [xnm])
            w_prefetch()
            items = []
            for j in range(4):
                xt = xbuf[:, xb_i, j, :]
                xnm = "x%d_%d" % (xb_i, j)
                items.append((xt, [xnm], h2T[:, xb_i, :, :], ["h2T%d" % xb_i], j * 128, None))
            norm_group(items, g2T, "colsA")
            if c + 1 < NCH:
                CP(HB[:, c + 1, :, 0:1], h2T[:, xb_i, :, 511:512], ["h2T%d" % xb_i], ["HB%d" % (c + 1)])
            if c >= 1:
                CP(HB[:, c - 1, :, 1:2], h2T[:, xb_i, :, 0:1], ["h2T%d" % xb_i], ["HB%d" % (c - 1)])

        def stage_F2(c):
            xb_i = c % 2
            h2 = h2T[:, xb_i, :, :]
            h2n = "h2T%d" % xb_i
            hbanks = [(psO, "psO"), (psD, "psD"), (psM, "psM"), (psTf, "psT")]
            for hf in range(2):
                fl = 0
                for i in range(6):
                    rg, rn, (_, f0, nf) = w_next("wup")
                    wv = rg.rearrange("p (k n) -> p k n", n=512)
                    w_prefetch()
                    for k in range(nf):
                        f = f0 + k
                        par = fl % 2
                        u_tiles = []
                        for which in range(2):
                            ui = f if which == 0 else NF + f
                            bank = psS[par][:, which * 512:(which + 1) * 512]
                            bn = "psS%d%s" % (par, "ab"[which])
                            lcol = which * 256 + k * 128
                            MMG([(bank, wv[:, dk, lcol:lcol + 128], h2[:, dk, :], dk == 0, dk == 7) for dk in range(8)],
                                [h2n, rn], [bn])
                            hb, hbn = hbanks[par * 2 + which]
                            hcol = hb[:, 0:2]
                            MMG([(hcol, wv[:, dk, lcol:lcol + 128], HB[:, c, dk, :], dk == 0, dk == 7) for dk in range(8)],
                                ["HB%d" % c, rn], [hbn])
                            ut = TF[:, par * 2 + which, :]
                            un = tfr(par * 2 + which)
                            w0 = colsB[:, ui:ui + 1]
                            w1 = colsB[:, 44 + ui:44 + ui + 1]
                            w2 = colsB[:, 88 + ui:88 + ui + 1]
                            bb = colsA[:, 19 + ui:19 + ui + 1]
                            ACT(ut, bank, AF.Identity, [bn, "colsA", "colsB"], [un], scale=w1, bias=bb)
                            STT(ut[:, 1:512], bank[:, 0:511], w0, ut[:, 1:512], ALU.mult, ALU.add, [bn, un, "colsB"], [un])
                            STT(ut[:, 0:1], hcol[:, 0:1], w0, ut[:, 0:1], ALU.mult, ALU.add, [hbn, un, "colsB"], [un])
                            STT(ut[:, 0:511], bank[:, 1:512], w2, ut[:, 0:511], ALU.mult, ALU.add, [bn, un, "colsB"], [un])
                            STT(ut[:, 511:512], hcol[:, 1:2], w2, ut[:, 511:512], ALU.mult, ALU.add, [hbn, un, "colsB"], [un])
                            u_tiles.append((ut, un))
                        (ug, ugn), (uv, uvn) = u_tiles
                        et = TF[:, 4 + par, :]
                        en = tfr(4 + par)
                        ACT(et, ug, AF.Silu, [ugn], [en])
                        TT(B16[:, fl, :], et, uv, ALU.mult, [en, uvn], ["b16_%d" % fl])
                        fl += 1
                dbanks = [(psS[0][:, 0:512], "psS0a"), (psS[0][:, 512:1024], "psS0b"),
                          (psS[1][:, 0:512], "psS1a"), (psS[1][:, 512:1024], "psS1b"),
                          (psO[:, :], "psO"), (psD[:, :], "psD"), (psM[:, :], "psM"), (psTf[:, :], "psT")]
                for i in range(3):
                    rg, rn, (_, f0, nf) = w_next("wdown")
                    wv = rg.rearrange("p (f n) -> p f n", n=1024)
                    w_prefetch()
                    fl0 = f0 - hf * 11
                    for j in range(4):
                        xt = xbuf[:, xb_i, j, :]
                        xnm = "x%d_%d" % (xb_i, j)
                        for hh in range(2):
                            bank, bn = dbanks[2 * j + hh]
                            MMG([(bank, B16[:, fl0 + k, j * 128:(j + 1) * 128], wv[:, k, hh * 512:(hh + 1) * 512],
                                  i == 0 and k == 0, i == 2 and k == nf - 1) for k in range(nf)],
                                ["b16_%d" % (fl0 + k) for k in range(nf)] + [rn], [bn])
                            if i == 2:
                                TT(xt[:, hh * 512:(hh + 1) * 512], xt[:, hh * 512:(hh + 1) * 512], bank, ALU.add,
                                   [xnm, bn], [xnm])
                        if hf == 1 and i == 2:
                            t = 4 * c + j
                            oi = 7 if j % 2 == 0 else 4
                            ot = TF[:, oi:oi + 2, :].rearrange("p a b -> p (a b)")
                            otn = [tfr(oi), tfr(oi + 1)]
                            jk, jkn = xns[j % 2]
                            cc, nm = rms_rstd(xt, D, 2 if j % 2 == 0 else 5, [xnm], jk[:], [jkn])
                            STT(ot, xt, cc, gf_b[:], ALU.mult, ALU.mult, [xnm, nm, "gf_b"], otn)
                            DMA("sp", "oout%d" % (j % 2), [(out_d[t * 128:(t + 1) * 128, :], ot)], otn, [], is_output=True)
                            if c + 2 < NCH:
                                t2 = 4 * (c + 2) + j
                                DMA("sp", "xin" + xnm, [(xt, x_d[t2 * 128:(t2 + 1) * 128, :])], [], [xnm])
                                preloaded.add((c + 2, j))

        stage_M(0)
        for c in range(NCH):
            if c + 1 < NCH:
                stage_M(c + 1)
            stage_F2(c)

        S.final_wait("sp")
        S.run(block)
    return nc


_CACHE = {}


def kernel(**inputs):
    x = np.ascontiguousarray(inputs["x"], dtype=np.float32)
    B, S_LEN, _ = x.shape
    NCH = S_LEN // 512
    if NCH not in _CACHE:
        _CACHE[NCH] = build(NCH)
    nc = _CACHE[NCH]
    consts = host_consts()
    shared = {}
    for k, v in inputs.items():
        if k == "x":
            continue
        a = np.ascontiguousarray(v)
        if k == "positions":
            a = a.astype(np.int32)
        else:
            a = a.astype(np.float32)
        shared[k] = a
    shared.update(consts)
    in_maps = []
    for b in range(B):
        m = dict(shared)
        m["x"] = x[b]
        in_maps.append(m)
    res = run_bass_kernel_spmd(nc, in_maps, core_ids=list(range(B)))
    out = np.stack([np.asarray(r["out"]) for r in res.results], axis=0)
    return out.astype(np.float32)
```

```python
import math
from contextlib import ExitStack

import numpy as np
import concourse.bass as bass
import concourse.mybir as mybir
from concourse.bass_utils import run_bass_kernel_spmd

F32 = mybir.dt.float32
BF16 = mybir.dt.bfloat16
I32 = mybir.dt.int32
AF = mybir.ActivationFunctionType
ALU = mybir.AluOpType

D = 1024
NDK = 8
H_A = 8
H_B = 16
KV_B = 4
D_FF = 2816
NF = 22
W_IN_COLS = 4032
C_QLAT, C_CKV, C_KR, C_QB, C_KB, C_VB, C_GA, C_GB = 0, 256, 384, 448, 1472, 1728, 1984, 3008
EPS = 1e-6
NEG = -30000.0
SC_A = 1.0 / math.sqrt(192.0)
SC_B = 1.0 / math.sqrt(64.0)
ENG_NAMES = ("pe", "act", "dve", "pool", "sp")
SAME_ENGINE_INORDER = ()


class Sched:
    def __init__(self, nc, sems, dma_sems):
        self.nc = nc
        self.sem = sems
        self.cnt = {e: 0 for e in ENG_NAMES}
        self.waited = {e: {} for e in ENG_NAMES}
        self.prog = {e: [] for e in ENG_NAMES}
        self.last_w = {}
        self.readers = {}
        self.dma_sems = dma_sems
        self.dma_slot = {}
        self.out_tokens = []

    def _need(self, eng, tok, waits):
        if tok is None:
            return
        key, sem, val = tok
        if key == eng and eng in SAME_ENGINE_INORDER:
            return
        if self.waited[eng].get(key, 0) >= val:
            return
        cur = waits.get(key)
        if cur is None or cur[1] < val:
            waits[key] = (sem, val)

    def _deps(self, eng, reads, writes):
        waits = {}
        for r in reads:
            self._need(eng, self.last_w.get(r), waits)
        for w in writes:
            self._need(eng, self.last_w.get(w), waits)
            for t in self.readers.get(w, ()):
                self._need(eng, t, waits)
        for key, (sem, val) in waits.items():
            self.waited[eng][key] = val
        return list(waits.values())

    def _commit(self, tok, reads, writes):
        for r in reads:
            lst = self.readers.setdefault(r, [])
            lst[:] = [t for t in lst if t[0] != tok[0]]
            lst.append(tok)
        for w in writes:
            self.last_w[w] = tok
            self.readers[w] = []

    def op(self, eng, fn, reads=(), writes=()):
        waits = self._deps(eng, reads, writes)
        self.cnt[eng] += 1
        val = self.cnt[eng]
        sem = self.sem[eng]
        tok = (eng, sem, val)

        def emit(e, waits=waits, fn=fn, sem=sem):
            for (s, v) in waits:
                e.wait_ge(s, v)
            fn(e).then_inc(sem, 1)

        self.prog[eng].append(emit)
        self._commit(tok, reads, writes)
        return tok

    def dma(self, eng, slot, fns, reads=(), writes=(), is_output=False):
        if slot not in self.dma_slot:
            self.dma_slot[slot] = [self.dma_sems.pop(), 0]
        st = self.dma_slot[slot]
        sem = st[0]
        waits = self._deps(eng, reads, writes)
        st[1] += 16 * len(fns)
        tok = ("dma:" + slot, sem, st[1])

        def emit(e, waits=waits, fns=fns, sem=sem):
            for (s, v) in waits:
                e.wait_ge(s, v)
            for f in fns:
                f(e).then_inc(sem, 16)

        self.prog[eng].append(emit)
        self._commit(tok, reads, writes)
        if is_output:
            self.out_tokens.append(tok)
        return tok

    def fence(self):
        toks = [(e, self.sem[e], self.cnt[e]) for e in ENG_NAMES if self.cnt[e] > 0]
        toks += [("dma:" + k, v[0], v[1]) for k, v in self.dma_slot.items() if v[1] > 0]
        for eng in ENG_NAMES:
            waits = {}
            for t in toks:
                if t[0] == eng:
                    continue
                self._need(eng, t, waits)
            for key, (sem, val) in waits.items():
                self.waited[eng][key] = val
            wl = list(waits.values())

            def emit(e, wl=wl):
                for (s_, v) in wl:
                    e.wait_ge(s_, v)

            self.prog[eng].append(emit)
        self.last_w.clear()
        self.readers.clear()

    def final_wait(self, eng):
        last = {}
        for (k, s, v) in self.out_tokens:
            if k not in last or last[k][1] < v:
                last[k] = (s, v)

        def emit(e, last=last):
            for (s, v) in last.values():
                e.wait_ge(s, v)

        self.prog[eng].append(emit)

    def run(self, block):
        prog = self.prog

        @block.tensor
        def _(e):
            for f in prog["pe"]:
                f(e)

        @block.scalar
        def _(e):
            for f in prog["act"]:
                f(e)

        @block.vector
        def _(e):
            for f in prog["dve"]:
                f(e)

        @block.gpsimd
        def _(e):
            for f in prog["pool"]:
                f(e)

        @block.sync
        def _(e):
            for f in prog["sp"]:
                f(e)


def t5_bucket_np(rel):
    nb = 16
    max_exact = 8
    base = (rel > 0).astype(np.int64) * nb
    n = np.abs(rel)
    nf = np.maximum(n, 1).astype(np.float32)
    large = max_exact + (np.log(nf / np.float32(max_exact)) / np.float32(math.log(128 / max_exact))
                         * np.float32(nb - max_exact)).astype(np.int64)
    large = np.minimum(large, nb - 1)
    return base + np.where(n < max_exact, n, large)


def host_consts():
    c = {}
    c["c_ident"] = np.eye(128, dtype=np.float32)
    c["c_J"] = np.eye(128, dtype=np.float32)[::-1].copy()
    G = np.zeros((33, 512), dtype=np.float32)
    for s in range(512):
        rel = 255 - s
        if abs(rel) <= 128 and s <= 510:
            G[int(t5_bucket_np(np.array([rel]))[0]), s] = 1.0
        else:
            G[32, s] = 1.0
    c["c_G"] = G
    half = 32
    inv = (np.float32(10000.0) ** (-np.arange(half, dtype=np.float32) / np.float32(half))).astype(np.float32)
    rope = np.zeros((64, 2), dtype=np.float32)
    rope[:, 0] = np.concatenate([inv, inv])
    rope[:, 1] = np.concatenate([-np.ones(half), np.ones(half)])
    c["c_rope"] = rope
    return c


def build(NCH, dbg=()):
    S_LEN = 512 * NCH
    NT = 4 * NCH
    nc = bass.Bass("TRN2", target_bir_lowering=False)

    def din(name, shape, dt=F32):
        return nc.dram_tensor(name, list(shape), dt, kind="ExternalInput").ap()

    x_d = din("x", [S_LEN, D])
    pos_d = din("positions", [S_LEN], I32)
    g1_d = din("norm1_g", [1, D])
    win_d = din("w_in", [1, D, W_IN_COLS])
    gq_d = din("q_a_norm_g", [1, 256])
    wqb_d = din("w_q_b", [1, 256, 1536])
    gkv_d = din("kv_a_norm_g", [1, 128])
    wkvb_d = din("w_kv_b", [1, 128, 2048])
    rb_d = din("rel_bias", [32, 16])
    sinks_d = din("sinks", [1, 16])
    wout_d = din("w_out", [1, D, D])
    g2_d = din("norm2_g", [1, D])
    wup_d = din("w_up", [1, D, 2 * D_FF])
    cw_d = din("conv_w", [1, 3, 2 * D_FF])
    cb_d = din("conv_b", [1, 2 * D_FF])
    wdn_d = din("w_down", [1, D_FF, D])
    gf_d = din("final_norm_g", [D])
    cid_d = din("c_ident", [128, 128])
    cJ_d = din("c_J", [128, 128])
    cG_d = din("c_G", [33, 512])
    crope_d = din("c_rope", [64, 2])
    out_d = nc.dram_tensor("out", [S_LEN, D], F32, kind="ExternalOutput").ap()
    uscr = nc.dram_tensor("uscr", [16, 512], F32, kind="Internal")
    rscr = nc.dram_tensor("rscr", [NCH, 64, 1024], F32, kind="Internal")
    tscr = nc.dram_tensor("tscr", [4, 128, 3072], BF16, kind="Internal")
    dbg_out = {}
    for (nm, shp) in dbg:
        dbg_out[nm] = nc.dram_tensor("dbg_" + nm, list(shp), F32, kind="ExternalOutput").ap()

    win_v = win_d[0].rearrange("(dk p) n -> p dk n", p=128)
    wout_v = wout_d[0].rearrange("(dk p) n -> p dk n", p=128)
    wup_v = wup_d[0].rearrange("(dk p) n -> p dk n", p=128)
    wdn_v = wdn_d[0].rearrange("(f p) n -> p f n", p=128)

    with ExitStack() as es:
        def sb(name, shape, dt):
            return es.enter_context(nc.sbuf_tensor(name, list(shape), dt))

        def psum(name, shape, dt):
            return es.enter_context(nc.psum_tensor(name, list(shape), dt))

        ckvnT = sb("ckvnT", [128, S_LEN], BF16)
        ckvn = sb("ckvn", [128, NT, 128], BF16)
        krT = sb("krT", [128, S_LEN], BF16)
        kbT = sb("kbT", [128, 2, S_LEN], BF16)
        vb = sb("vb", [128, NT, 256], BF16)
        Wqabs = sb("Wqabs", [128, 2, 8, 128], BF16)
        Wqr = sb("Wqr", [128, 2, 8, 128], BF16)
        Wvb = sb("Wvb", [128, 8, 128], BF16)
        identb = sb("identb", [128, 128], BF16)
        onesb = sb("onesb", [128, 128], BF16)
        onesf = sb("onesf", [128, 128], F32)
        identf = sb("identf", [128, 128], F32)
        Jf = sb("Jf", [128, 128], F32)
        colsA = sb("colsA", [128, 64], F32)
        colsB = sb("colsB", [128, 132], F32)
        gf_b = sb("gf_b", [128, D], F32)
        sinkexp = sb("sinkexp", [128, 16], F32)
        ropec = sb("ropec", [64, 2], F32)
        epsc = sb("epsc", [128, 1], F32)
        halfpi = sb("halfpi", [128, 1], F32)
        ss = sb("ss", [128, 8], F32)
        HB = sb("HB", [128, NCH, 8, 2], BF16)
        xbuf = sb("xbuf", [128, 2, 4, D], F32)
        h2T = sb("h2T", [128, 2, 8, 512], BF16)
        ring = sb("ring", [128, 2, 4096], BF16)
        B16 = sb("B16", [128, 24, 512], BF16)
        TF = sb("TF", [128, 9, 512], F32)
        xn = sb("xn", [128, D], BF16)
        qlnT = sb("qlnT", [128, 2, 512], BF16)
        qabsT = sb("qabsT", [128, 2, 512], BF16)
        qrT = sb("qrT", [128, 2, 512], BF16)
        PT = sb("PT", [128, 2, 1024], BF16)
        Tbuf = sb("Tbuf", [128, 2, 3, 512], BF16)
        sk2 = sb("sk2", [128, 8], F32)
        xn2 = sb("xn2", [128, D], BF16)
        rtab = sb("rtab", [64, 2, 512], F32)

        rtmp = TF[0:64, 0:3, :]
        rtmi = sb("rtmi", [64, 512], I32)

        psS = [psum("psS0", [128, 1024], F32), psum("psS1", [128, 1024], F32)]
        psO = psum("psO", [128, 512], F32)
        psD = psum("psD", [128, 512], F32)
        psM = psum("psM", [128, 512], F32)
        psT = psum("psT", [128, 1024], BF16)
        psTf = psT[:].bitcast(F32)

        sems = {e: es.enter_context(nc.semaphore("s_" + e)) for e in ENG_NAMES}
        dsems = [es.enter_context(nc.semaphore("d%d" % i)) for i in range(34)]
        block = es.enter_context(nc.Block())
        S = Sched(nc, sems, dsems)

        def ACT(out, in_, func, reads, writes, **kw):
            return S.op("act", lambda e: e.activation(out=out, in_=in_, func=func, **kw), reads, writes)

        def TT(out, in0, in1, op, reads, writes, eng="dve"):
            return S.op(eng, lambda e: e.tensor_tensor(out=out, in0=in0, in1=in1, op=op), reads, writes)

        def TS(out, in0, s1, s2, op0, op1, reads, writes, eng="dve"):
            if s2 is None:
                return S.op(eng, lambda e: e.tensor_scalar(out=out, in0=in0, scalar1=s1, scalar2=None, op0=op0), reads, writes)
            return S.op(eng, lambda e: e.tensor_scalar(out=out, in0=in0, scalar1=s1, scalar2=s2, op0=op0, op1=op1), reads, writes)

        def STT(out, in0, scalar, in1, op0, op1, reads, writes, eng="dve"):
            return S.op(eng, lambda e: e.scalar_tensor_tensor(out=out, in0=in0, scalar=scalar, in1=in1, op0=op0, op1=op1), reads, writes)

        def CP(out, in_, reads, writes, eng="dve"):
            return S.op(eng, lambda e: e.tensor_copy(out=out, in_=in_), reads, writes)

        def RCP(out, in_, reads, writes):
            return S.op("dve", lambda e: e.reciprocal(out=out, in_=in_), reads, writes)

        def MSET(ap, val, writes, eng="pool"):
            return S.op(eng, lambda e: e.memset(ap, val), (), writes)

        def MMG(mms, reads, writes):
            def fn(e, mms=mms):
                ins = None
                for (o, l, r, st, sp) in mms:
                    ins = e.matmul(o, lhsT=l, rhs=r, start=st, stop=sp)
                return ins
            return S.op("pe", fn, reads, writes)

        def TRS(items, reads, writes):
            def fn(e, items=items):
                ins = None
                for (o, i) in items:
                    ins = e.transpose(o, i, identb[:])
                return ins
            return S.op("pe", fn, list(reads) + ["identb"], writes)

        def DMA(eng, slot, pairs, reads, writes, is_output=False):
            fns = [(lambda e, o=o, i=i: e.dma_start(out=o, in_=i)) for (o, i) in pairs]
            return S.dma(eng, slot, fns, reads, writes, is_output=is_output)

        def bc_last(ap, n):
            return bass.AP(ap.tensor, ap.offset, list(ap.ap) + [[0, n]])

        HT = ["b16_%d" % i for i in range(8)]
        MIX = ["b16_%d" % i for i in range(8, 16)]
        QW = ["b16_%d" % i for i in range(16, 24)]
        hT = B16[:, 0:8, :]

        def tfr(i):
            return "tf%d" % i

        MSET(epsc[:], EPS, ["epsc"])
        MSET(halfpi[:], math.pi / 2, ["halfpi"])
        MSET(onesb[:], 1.0, ["onesb"])
        MSET(onesf[:], 1.0, ["onesf"])
        MSET(HB[:], 0.0, ["HB%d" % c for c in range(NCH)])
        MSET(ss[:], 0.0, ["ss"])
        MSET(krT[64:128, :], 0.0, ["krT"])
        MSET(qrT[64:128, :, :], 0.0, ["qrT0", "qrT1"])
        DMA("sp", "c0", [(identf[:], cid_d[:, :]), (Jf[:], cJ_d[:, :]), (ropec[:], crope_d[:, :]),
                         (gf_b[:], gf_d.partition_broadcast(128)),
                         (sinkexp[:], sinks_d[0].partition_broadcast(128))],
            [], ["identf", "Jf", "ropec", "gf_b", "sinkexp"])
        CP(identb[:], identf[:], ["identf"], ["identb"])
        ACT(sinkexp[:], sinkexp[:], AF.Exp, ["sinkexp"], ["sinkexp"])

        if True:
            wqbf = xbuf[:, 0, 0:3, :].rearrange("p a n -> p (a n)").rearrange("p (a n) -> p a n", a=2)
            wkvf = xbuf[:, 1, 0:2, :].rearrange("p a n -> p (a n)")
            rowsA = TF[0:64, 0, 0:128]
            rowsB = TF[:, 0, 128:256]
            rowsC = TF[0:4, 0, 256:384]
            ATs = [(TF[:, 1, 0:256].rearrange("p (a b) -> p a b", b=128), "AT0"),
                   (TF[:, 2, 128:384].rearrange("p (a b) -> p a b", b=128), "AT1")]
            BTs = [(TF[:, 1, 256:384], "BT0"), (TF[:, 2, 384:512], "BT1")]
            rbx = TF[0:33, 2, 0:16]
            Gs = TF[0:33, 3, :]
            us = TF[0:16, 4, :]
            Tt = TF[:, 5:8, :].rearrange("p a b -> p (a b)").rearrange("p (a b c) -> p a b c", a=3, b=4)
            Hk = B16[:].rearrange("p a b -> p (a b)").bitcast(F32).rearrange("p (a b c) -> p a b c", a=3, b=16)

            sbk = [(psM, "psM"), (psO, "psO"), (psD, "psD"), (psS[0], "psS0a"), (psS[1], "psS1a")]
            srot = {"i": 0}

            def MMC(out_sl, lhsT, rhs, rreads, dst, dst_reads, dst_writes, rows=slice(0, 128), fn=None):
                bk, bn = sbk[srot["i"] % len(sbk)]
                srot["i"] += 1
                pv_ = bk[rows, out_sl]
                MMG([(pv_, lhsT, rhs, True, True)], rreads, [bn])
                if fn is None:
                    CP(dst, pv_, [bn] + list(dst_reads), dst_writes)
                else:
                    fn(pv_, bn)

            MSET(rowsA, 0.0, ["rowsA"])
            cwrows = cw_d[0].rearrange("t (f p) -> (t f) p", p=128)
            DMA("sp", "c1", [(rowsA[0:8, :], g1_d[0].rearrange("(a p) -> a p", p=128)),
                             (rowsA[8:16, :], g2_d[0].rearrange("(a p) -> a p", p=128)),
                             (rowsA[16:18, :], gq_d[0].rearrange("(a p) -> a p", p=128)),
                             (rowsA[18:19, :], gkv_d[0].rearrange("(a p) -> a p", p=128)),
                             (rowsA[19:63, :], cb_d[0].rearrange("(a p) -> a p", p=128)),
                             (rowsB, cwrows[0:128, :]),
                             (rowsC, cwrows[128:132, :])],
                ["rowsA"], ["rowsA", "rowsB", "rowsC"])
            MMC(slice(0, 64), rowsA, identf[0:64, 0:64], ["rowsA", "identf"], colsA[:], [], ["colsA"])
            MMC(slice(0, 128), rowsB, identf[:, :], ["rowsB", "identf"], colsB[:, 0:128], [], ["colsB"])
            MMC(slice(0, 4), rowsC, identf[0:4, 0:4], ["rowsC", "identf"], colsB[:, 128:132], ["colsB"], ["colsB"])
            g1T = colsA[:, 0:8]
            g2T = colsA[:, 8:16]
            gqT = colsA[:, 16:18]
            gkvT = colsA[:, 18:19]

            MSET(rbx, NEG, ["rbx"])
            DMA("sp", "c3", [(rbx[0:32, :], rb_d[:, :]), (Gs, cG_d[:, :])], ["rbx"], ["rbx", "Gs"])
            MMG([(psM[0:16, :], rbx, Gs, True, True)], ["rbx", "Gs"], ["psM"])
            CP(us, psM[0:16, :], ["psM"], ["us"])
            DMA("sp", "c4", [(uscr.ap()[:, :], us)], ["us"], ["uscr"])
            base_tt = (256, 128, 0)
            pairs = []
            for tt in range(3):
                for h in range(16):
                    gg, jj = divmod(h, 4)
                    hpos = 4 * gg + (jj % 2) * 2 + jj // 2
                    pairs.append((Hk[:, tt, hpos, :], bass.AP(uscr, h * 512 + base_tt[tt], [[1, 128], [1, 128]])))
            DMA("sp", "c5", pairs, ["uscr"], ["Hk"])
            DMA("sp", "c2", [(wqbf, wqb_d[0].rearrange("(a p) n -> p a n", p=128)),
                             (wkvf, wkvb_d[0])], [], ["wqbf", "wkvf"])
            for a in range(2):
                TS(wqbf[:, a, :], wqbf[:, a, :], gqT[:, a:a + 1], None, ALU.mult, None, ["wqbf", "colsA"], ["wqbf"])
            TS(wkvf, wkvf, gkvT, None, ALU.mult, None, ["wkvf", "colsA"], ["wkvf"])
            for h in range(H_A):
                for a in range(2):
                    b0 = h * 192 + 128
                    CP(Wqr[:, a, h, 0:64], wqbf[:, a, b0:b0 + 64], ["wqbf"], ["Wqr"])
                    CP(Wqr[:, a, h, 64:96], wqbf[:, a, b0 + 32:b0 + 64], ["wqbf", "Wqr"], ["Wqr"])
                    CP(Wqr[:, a, h, 96:128], wqbf[:, a, b0:b0 + 32], ["wqbf", "Wqr"], ["Wqr"])
                CP(Wvb[:, h, :], wkvf[:, h * 256 + 128:h * 256 + 256], ["wkvf"], ["Wvb"], eng="pool")
                ATh = ATs[h % 2]
                BTh = BTs[h % 2]
                for a in range(2):
                    MMC(slice(0, 128), wqbf[:, a, h * 192:h * 192 + 128], identf[:, :], ["wqbf", "identf"],
                        ATh[0][:, a, :], [], [ATh[1]])
                MMC(slice(0, 128), wkvf[:, h * 256:h * 256 + 128], identf[:, :], ["wkvf", "identf"], BTh[0], [], [BTh[1]])
                for a in range(2):
                    MMC(slice(0, 128), ATh[0][:, a, :], BTh[0], [ATh[1], BTh[1]], Wqabs[:, a, h, :], [], ["Wqabs"])

            Thl = ring[:, 0, 0:3072].rearrange("p (a b c) -> p a b c", a=2, b=3)
            tmpT = TF[:, 8, :]
            for g in range(4):
                for tt in range(3):
                    MMG([(psM[:, :], Jf[:, :], Hk[:, tt, 4 * g:4 * g + 4, :].rearrange("p a b -> p (a b)"), True, True)],
                        ["Jf", "Hk"], ["psM"])
                    t8 = Tt[:, tt, :, :].rearrange("p a b -> p (a b)")
                    TS(t8, psM[:, :], 1.0 / SC_B, None, ALU.mult, None, ["psM"], ["Tt"])
                    CP(Thl[:, 0, tt, :], t8, ["Tt"], ["Thl"])
                    CP(tmpT, Thl[:, 0, tt, :], ["Thl"], ["tmpT"])
                    TT(tmpT, t8, tmpT, ALU.subtract, ["Tt", "tmpT"], ["tmpT"])
                    CP(Thl[:, 1, tt, :], tmpT, ["tmpT", "Thl"], ["Thl"])
                DMA("sp", "c6", [(tscr.ap()[g], ring[:, 0, 0:3072])], ["Thl"], ["tscr"])
            CP(sk2[0:64, :], sinkexp[0:64, 0:16:2], ["sinkexp"], ["sk2"])
            CP(sk2[64:128, :], sinkexp[64:128, 1:16:2], ["sinkexp", "sk2"], ["sk2"])

        def rms_rstd(src, nfeat, col, src_reads, junk, junk_names):
            c = ss[:, col:col + 1]
            nm = "ss%d" % col
            ACT(junk, src, AF.Square, list(src_reads) + [nm], list(junk_names) + [nm], accum_out=c)
            ACT(c, c, AF.Ln, [nm, "epsc"], [nm], scale=1.0 / nfeat, bias=epsc[:, 0:1])
            ACT(c, c, AF.Exp, [nm], [nm], scale=-0.5)
            return c, nm

        psMb = psM[:].bitcast(BF16)
        xns = [(xn, "xn"), (xn2, "xn2")]
        trb = [(psT[:, :], "psT"), (psMb, "psM")]

        def norm_group(items, gT, gname):
            n = len(items)
            cols = {}

            def S_(k):
                xtile, xreads, dst, dst_names, col0, pre = items[k]
                if pre is not None:
                    pre()
                cols[k] = rms_rstd(xtile, D, 0 if k % 2 == 0 else 3, xreads,
                                   TF[:, 7:9, :].rearrange("p a b -> p (a b)"), [tfr(7), tfr(8)])

            def X_(k):
                xtile, xreads, dst, dst_names, col0, pre = items[k]
                c, nm = cols[k]
                xb_, xbn = xns[k % 2]
                TS(xb_[:], xtile, c, None, ALU.mult, None, list(xreads) + [nm], [xbn])

            def T_(k):
                xb_, xbn = xns[k % 2]
                tb, tbn = trb[k % 2]
                TRS([(tb[:, dk * 128:(dk + 1) * 128], xb_[:, dk * 128:(dk + 1) * 128]) for dk in range(8)], [xbn], [tbn])

            def E_(k):
                xtile, xreads, dst, dst_names, col0, pre = items[k]
                tb, tbn = trb[k % 2]
                TT(dst[:, :, col0:col0 + 128], tb.rearrange("p (a b) -> p a b", b=128), bc_last(gT, 128), ALU.mult,
                   [tbn, gname], dst_names)

            for k in range(n):
                S_(k)
                X_(k)
                T_(k)
                if k >= 1:
                    E_(k - 1)
            E_(n - 1)

        def rope_tables(c):
            DMA("sp", "pos", [(rtmi[:], pos_d[c * 512:(c + 1) * 512].partition_broadcast(64))], [], ["rtmi"])
            a, n, t = rtmp[:, 0, :], rtmp[:, 1, :], rtmp[:, 2, :]
            CP(a, rtmi[:], ["rtmi"], ["tf0"])
            TS(a, a, ropec[:, 0:1], None, ALU.mult, None, ["tf0", "ropec"], ["tf0"])
            TS(n, a, float(1.0 / (2 * math.pi)), None, ALU.mult, None, ["tf0"], ["tf1"])
            CP(rtmi[:], n, ["tf1"], ["rtmi"])
            CP(n, rtmi[:], ["rtmi"], ["tf1"])
            STT(a, n, -6.28125, a, ALU.mult, ALU.add, ["tf1", "tf0"], ["tf0"])
            STT(a, n, -float(2 * math.pi - 6.28125), a, ALU.mult, ALU.add, ["tf1", "tf0"], ["tf0"])
            TS(n, a, float(math.pi), float(-2 * math.pi), ALU.is_gt, ALU.mult, ["tf0"], ["tf1"])
            TT(a, a, n, ALU.add, ["tf0", "tf1"], ["tf0"])
            TS(n, a, float(-math.pi), float(2 * math.pi), ALU.is_lt, ALU.mult, ["tf0"], ["tf1"])
            TT(a, a, n, ALU.add, ["tf0", "tf1"], ["tf0"])
            ACT(t, a, AF.Sin, ["tf0"], ["tf2"])
            TS(rtab[:, 1, :], t, ropec[:, 1:2], None, ALU.mult, None, ["tf2", "ropec"], ["rtab1"])
            STT(n, a, -1.0, a, ALU.mult, ALU.max, ["tf0"], ["tf1"])
            ACT(rtab[:, 0, :], n, AF.Sin, ["tf1", "halfpi"], ["rtab0"], scale=-1.0, bias=halfpi[0:64, :])

        def rope_apply(psA, psA_names, psB, psB_names, dst, dst_names):
            t0, t1 = rtmp[:, 0, :], rtmp[:, 1, :]
            TT(t0, psA, rtab[:, 0, :], ALU.mult, list(psA_names) + ["rtab0"], ["tf0"])
            TT(t1, psB, rtab[:, 1, :], ALU.mult, list(psB_names) + ["rtab1"], ["tf1"])
            TT(dst, t0, t1, ALU.add, ["tf0", "tf1"], dst_names)

        def gen_pieces():
            seq = []

            def m_pieces():
                seq.extend([("win", C_QLAT, 256), ("win", C_QB, 512), ("win", C_GB, 512),
                            ("win", C_QB + 512, 512), ("win", C_GB + 512, 512),
                            ("win", C_GA, 512), ("win", C_GA + 512, 512),
                            ("wout", 0, 512), ("wout", 512, 512)])

            def f_pieces():
                for hf in range(2):
                    for i in range(6):
                        seq.append(("wup", hf * 11 + 2 * i, 1 if i == 5 else 2))
                    for i in range(3):
                        seq.append(("wdown", hf * 11 + 4 * i, 3 if i == 2 else 4))

            m_pieces()
            for c in range(NCH):
                if c + 1 < NCH:
                    m_pieces()
                f_pieces()
            return seq

        pieces = gen_pieces()
        wstate = {"next_issue": 0, "next_use": 0}

        def w_issue():
            i = wstate["next_issue"]
            if i >= len(pieces):
                return
            wstate["next_issue"] = i + 1
            slot = i % 2
            kind, a, b = pieces[i]
            rg = ring[:, slot, :]
            nm = "ring%d" % slot
            if kind == "win":
                pr = [(rg[:, 0:8 * b].rearrange("p (k n) -> p k n", n=b), win_v[:, :, a:a + b])]
            elif kind == "wout":
                pr = [(rg[:, 0:8 * b].rearrange("p (k n) -> p k n", n=b), wout_v[:, :, a:a + b])]
            elif kind == "wup":
                v = rg.rearrange("p (k n) -> p k n", n=512)
                pr = [(v[:, :, 0:128 * b], wup_v[:, :, a * 128:(a + b) * 128]),
                      (v[:, :, 256:256 + 128 * b], wup_v[:, :, D_FF + a * 128:D_FF + (a + b) * 128])]
            else:
                v = rg.rearrange("p (f n) -> p f n", n=1024)
                pr = [(v[:, 0:b, :], wdn_v[:, a:a + b, :])]
            DMA("pool", nm, pr, [], [nm])

        def w_next(kind):
            i = wstate["next_use"]
            wstate["next_use"] = i + 1
            assert pieces[i][0] == kind, (pieces[i], kind)
            while wstate["next_issue"] <= i:
                w_issue()
            return ring[:, i % 2, :], "ring%d" % (i % 2), pieces[i]

        def w_prefetch():
            if wstate["next_issue"] <= wstate["next_use"]:
                w_issue()

        S.fence()
        MSET(B16[:, 16:24, :], 0.0, QW, eng="dve")
        if True:
            Wp1 = ring[:].rearrange("p a b -> p (a b)")[:, 0:6144].rearrange("p (k n) -> p k n", n=768)
            DMA("pool", "wp1", [(Wp1[:, :, 0:128], win_v[:, :, C_CKV:C_CKV + 128]),
                                (Wp1[:, :, 128:384], win_v[:, :, C_VB:C_VB + 256]),
                                (Wp1[:, :, 384:448], win_v[:, :, C_KR:C_KR + 64]),
                                (Wp1[:, :, 448:480], win_v[:, :, C_KR + 32:C_KR + 64]),
                                (Wp1[:, :, 480:512], win_v[:, :, C_KR:C_KR + 32]),
                                (Wp1[:, :, 512:768], win_v[:, :, C_KB:C_KB + 256])], [], ["Wp1"])
            hTs = [(B16[:, 0:8, :], HT), (B16[:, 8:16, :], MIX)]

            def p1_norms(c):
                hTc, HTc = hTs[c % 2]
                items = []
                for j in range(4):
                    t = 4 * c + j
                    xt = xbuf[:, c % 2, j, :]
                    xnm = "x%d_%d" % (c % 2, j)
                    DMA("sp", "xin" + xnm, [(xt, x_d[t * 128:(t + 1) * 128, :])], [], [xnm])
                    items.append((xt, [xnm], hTc, HTc, j * 128, None))
                norm_group(items, g1T, "colsA")

            def p1_proj(c):
                hT, HT = hTs[c % 2]
                MMG([(psO[:, :], Wp1[:, dk, 384:512], hT[:, dk, :], dk == 0, dk == 7) for dk in range(8)],
                    HT + ["Wp1"], ["psO"])
                rope_apply(psO[0:64, :], ["psO"], psO[64:128, :], ["psO"], krT[0:64, c * 512:(c + 1) * 512], ["krT"])
                for m in range(2):
                    bank = psS[m][:, 0:512]
                    bn = "psS%da" % m
                    MMG([(bank, Wp1[:, dk, 512 + m * 128:512 + (m + 1) * 128], hT[:, dk, :], dk == 0, dk == 7)
                         for dk in range(8)], HT + ["Wp1"], [bn])
                    ACT(kbT[:, m, c * 512:(c + 1) * 512], bank, AF.Copy, [bn], ["kbT"])
                for j in range(4):
                    t = 4 * c + j
                    par = j % 2
                    bk = psS[par][:, 512:1024]
                    bn = "psS%db" % par
                    MMG([(bk[:, 0:384], hT[:, dk, j * 128:(j + 1) * 128], Wp1[:, dk, 0:384], dk == 0, dk == 7)
                         for dk in range(8)], HT + ["Wp1"], [bn])
                    cc, nm = rms_rstd(bk[:, 0:128], 128, 1 if par == 0 else 4, [bn], TF[:, 8, 0:128], [tfr(8)])
                    TS(ckvn[:, t, :], bk[:, 0:128], cc, None, ALU.mult, None, [bn, nm], ["ckvn"])
                    ACT(vb[:, t, :], bk[:, 128:384], AF.Copy, [bn], ["vb"])
                    tb, tbn = trb[par]
                    TRS([(tb[:, 0:128], ckvn[:, t, :])], ["ckvn"], [tbn])
                    CP(ckvnT[:, t * 128:(t + 1) * 128], tb[:, 0:128], [tbn], ["ckvnT"])

            p1_norms(0)
            rope_tables(0)
            for c in range(NCH):
                if c + 1 < NCH:
                    p1_norms(c + 1)
                p1_proj(c)
                if c + 1 < NCH:
                    rope_tables(c + 1)
        S.fence()

        qw_v = B16[:, 16:24, :].rearrange("p a b -> p (a b)").rearrange("p (g t j q) -> p g t j q", g=2, t=4, j=4)
        mixT = B16[:, 8:16, :]

        def sigmoid_den(dst, src_ps, src_names, dst_name):
            ACT(dst, src_ps, AF.Exp, src_names, [dst_name], scale=-1.0)
            ACT(dst, dst, AF.Ln, [dst_name], [dst_name], bias=1.0)
            ACT(dst, dst, AF.Exp, [dst_name], [dst_name], scale=-1.0)

        def recip_act(dst, src, src_names, dst_names):
            ACT(dst, src, AF.Ln, src_names, dst_names)
            ACT(dst, dst, AF.Exp, dst_names, dst_names, scale=-1.0)

        def stage_M(c):
            xb_i = c % 2
            rope_tables(c)
            items = []
            for j in range(4):
                t = 4 * c + j
                xt = xbuf[:, xb_i, j, :]
                xnm = "x%d_%d" % (xb_i, j)
                DMA("sp", "xin" + xnm, [(xt, x_d[t * 128:(t + 1) * 128, :])], [], [xnm])
                items.append((xt, [xnm], hT, HT, j * 128, None))
            norm_group(items, g1T, "colsA")
            rg, rn, _ = w_next("win")
            w_prefetch()
            wq = rg[:, 0:8 * 256].rearrange("p (k n) -> p k n", n=256)
            qcols = {}

            def qMM(j):
                bk, bn = (psO, "psO") if j % 2 == 0 else (psD, "psD")
                MMG([(bk[:, 0:256], hT[:, dk, j * 128:(j + 1) * 128], wq[:, dk, :], dk == 0, dk == 7) for dk in range(8)],
                    HT + [rn], [bn])

            def qSX(j):
                bk, bn = (psO, "psO") if j % 2 == 0 else (psD, "psD")
                cc, nm = rms_rstd(bk[:, 0:256], 256, 1 if j % 2 == 0 else 4, [bn], TF[:, 8, 0:256], [tfr(8)])
                xb_, xbn = xns[j % 2]
                TS(xb_[:, 0:256], bk[:, 0:256], cc, None, ALU.mult, None, [bn, nm], [xbn])

            def qT(j):
                xb_, xbn = xns[j % 2]
                tb, tbn = trb[j % 2]
                TRS([(tb[:, a * 128:(a + 1) * 128], xb_[:, a * 128:(a + 1) * 128]) for a in range(2)], [xbn], [tbn])

            def qE(j):
                tb, tbn = trb[j % 2]
                CP(qlnT[:, :, j * 128:(j + 1) * 128], tb[:, 0:256].rearrange("p (a b) -> p a b", b=128), [tbn], ["qlnT"])

            qMM(0)
            for j in range(4):
                if j + 1 < 4:
                    qMM(j + 1)
                qSX(j)
                qT(j)
                if j >= 1:
                    qE(j - 1)
            qE(3)

            wqb = [None, None]
            rg, rn, _ = w_next("win")
            wqb[0] = (rg[:, 0:4096].rearrange("p (k n) -> p k n", n=512), rn)
            w_prefetch()
            gB = [None, None]

            rot = {"q": 0, "s": 0}
            qbanks = [(psM[:, :], "psM"), (psTf[:, :], "psT"), (psO[:, :], "psO"), (psD[:, :], "psD")]
            sbanks = [(psS[0][:, 0:512], "psS0a"), (psS[1][:, 0:512], "psS1a"),
                      (psS[0][:, 512:1024], "psS0b"), (psS[1][:, 512:1024], "psS1b")]

            def win_pass(gp):
                wv, rn = wqb[gp]
                for gl in range(2):
                    g = 2 * gp + gl
                    half = g % 2
                    for jp in range(2):
                        hq = g * 4 + 2 * jp
                        colb = (hq * 64) % 512
                        bk, bn = qbanks[rot["q"] % 4]
                        rot["q"] += 1
                        MMG([(bk, wv[:, dk, colb:colb + 128], hT[:, dk, :], dk == 0, dk == 7) for dk in range(8)],
                            HT + [rn], [bn])
                        for e in range(2):
                            jj = 2 * jp + e
                            jpos = (jj % 2) * 2 + jj // 2
                            src = bk[e * 64:e * 64 + 64, :]
                            ACT(qw_v[half * 64:half * 64 + 64, gl, :, jpos, :], src.rearrange("p (t q) -> p t q", q=128),
                                AF.Copy, [bn], QW)

            def gate_sig(wv, rn, fcl, tfi):
                bk, bn = qbanks[rot["q"] % 2]
                rot["q"] += 1
                MMG([(bk, wv[:, dk, fcl * 128:(fcl + 1) * 128], hT[:, dk, :], dk == 0, dk == 7) for dk in range(8)],
                    HT + [rn], [bn])
                sigmoid_den(TF[:, tfi, :], bk, [bn], tfr(tfi))
                return (TF[:, tfi, :], tfr(tfi))

            ptsets = [[(PT[:, 0, 0:512], "PT0a"), (PT[:, 0, 512:1024], "PT0b"),
                       (TF[:, 0, :].bitcast(BF16)[:, 0:512], tfr(0))],
                      [(PT[:, 1, 0:512], "PT1a"), (PT[:, 1, 512:1024], "PT1b"),
                       (TF[:, 1, :].bitcast(BF16)[:, 0:512], tfr(1))]]
            odsets = [((psO, "psO"), (psD, "psD")), ((psM, "psM"), (psTf, "psT"))]

            def win_attend_pass(gp):
                its = []
                for gl in range(2):
                    g = 2 * gp + gl
                    for t4 in range(4):
                        its.append((g, gl, t4))
                state = {"g": None}

                def front(n, g, gl, t4):
                    if state["g"] != g:
                        state["g"] = g
                        for k in range(2):
                            gate_sig(gB[gp][0], gB[gp][1], 2 * gl + k, 5 + k)
                        DMA("sp", "tld", [(Tbuf[:].rearrange("p a b c -> p (a b c)"), tscr.ap()[g])], ["tscr"], ["Tbuf"])
                    t = 4 * c + t4
                    kts = [kt for kt in (t - 1, t, t + 1) if 0 <= kt < NT]
                    rhs_q = qw_v[:, gl, t4, :, :].rearrange("p j q -> p (j q)")
                    for idx, kt in enumerate(kts):
                        tt = kt - t + 1
                        bk, bn = sbanks[rot["s"] % 4]
                        rot["s"] += 1
                        MMG([(bk, kbT[:, g // 2, kt * 128:(kt + 1) * 128], rhs_q, True, False),
                             (bk, identb[:, :], Tbuf[:, 0, tt, :], False, False),
                             (bk, identb[:, :], Tbuf[:, 1, tt, :], False, True)],
                            ["kbT", "identb", "Tbuf"] + QW, [bn])
                        pt, pn = ptsets[n % 2][idx]
                        ACT(pt, bk, AF.Exp, [bn], [pn], scale=SC_B)
                    return kts

                def back(n, g, gl, t4, kts):
                    (Ob, On), (Db, Dn) = odsets[n % 2]
                    mmo, mmd, rd = [], [], []
                    L = len(kts)
                    for idx, kt in enumerate(kts):
                        pt, pn = ptsets[n % 2][idx]
                        rd.append(pn)
                        vv = vb[:, kt, g * 64:(g + 1) * 64]
                        mmo.append((Ob[0:64, 0:256], vv, pt[:, 0:256], idx == 0, idx == L - 1))
                        mmd.append((Db[0:64, 0:256], onesb[:, 0:64], pt[:, 0:256], idx == 0, idx == L - 1))
                    for idx, kt in enumerate(kts):
                        pt, pn = ptsets[n % 2][idx]
                        vv = vb[:, kt, g * 64:(g + 1) * 64]
                        mmo.append((Ob[64:128, 0:256], vv, pt[:, 256:512], idx == 0, idx == L - 1))
                        mmd.append((Db[64:128, 0:256], onesb[:, 0:64], pt[:, 256:512], idx == 0, idx == L - 1))
                    MMG(mmo, rd + ["vb"], [On])
                    MMG(mmd, rd + ["onesb"], [Dn])
                    den = TF[:, 3, 0:256]
                    TT(den.rearrange("p (k q) -> p k q", q=128), Db[:, 0:256].rearrange("p (k q) -> p k q", q=128),
                       bc_last(sk2[:, 2 * g:2 * g + 2], 128), ALU.add, [Dn, "sk2"], [tfr(3)])
                    recip_act(den, den, [tfr(3)], [tfr(3)])
                    ob = TF[:, 4, 0:256]
                    TT(ob, Ob[:, 0:256], den, ALU.mult, [On, tfr(3)], [tfr(4)])
                    TT(mixT[:, 2 * g:2 * g + 2, t4 * 128:(t4 + 1) * 128], ob.rearrange("p (k q) -> p k q", q=128),
                       TF[:, 5:7, t4 * 128:(t4 + 1) * 128], ALU.mult, [tfr(4), tfr(5), tfr(6)], [MIX[2 * g], MIX[2 * g + 1]])

                prev = None
                for n, (g, gl, t4) in enumerate(its):
                    if prev is not None and prev[1] != g:
                        back(*prev)
                        prev = None
                    kts = front(n, g, gl, t4)
                    if prev is not None:
                        back(*prev)
                    prev = (n, g, gl, t4, kts)
                back(*prev)

            win_pass(0)
            rg, rn, _ = w_next("win")
            gB[0] = (rg[:, 0:4096].rearrange("p (k n) -> p k n", n=512), rn)
            w_prefetch()
            win_attend_pass(0)
            rg, rn, _ = w_next("win")
            wqb[1] = (rg[:, 0:4096].rearrange("p (k n) -> p k n", n=512), rn)
            w_prefetch()
            win_pass(1)
            rg, rn, _ = w_next("win")
            gB[1] = (rg[:, 0:4096].rearrange("p (k n) -> p k n", n=512), rn)
            w_prefetch()
            win_attend_pass(1)

            def q_project(h, qb):
                MMG([(psM[:, :], Wqabs[:, a, h, :], qlnT[:, a, :], a == 0, a == 1) for a in range(2)],
                    ["Wqabs", "qlnT"], ["psM"])
                ACT(qabsT[:, qb, :], psM[:, :], AF.Copy, ["psM"], ["qabsT%d" % qb])
                MMG([(psTf[:, :], Wqr[:, a, h, :], qlnT[:, a, :], a == 0, a == 1) for a in range(2)],
                    ["Wqr", "qlnT"], ["psT"])
                rope_apply(psTf[0:64, :], ["psT"], psTf[64:128, :], ["psT"], qrT[0:64, qb, :], ["qrT%d" % qb])

            gA = [None, None]
            pend = {"fin": None}
            heads = []
            q_project(0, 0)
            NP = NT // 2
            for h in range(H_A):
                qb = h % 2
                def qk(p, h=h, qb=qb):
                    for u in range(2):
                        kt = 2 * p + u
                        bank = psS[p % 2][:, u * 512:(u + 1) * 512]
                        MMG([(bank, ckvnT[:, kt * 128:(kt + 1) * 128], qabsT[:, qb, :], True, False),
                             (bank, krT[:, kt * 128:(kt + 1) * 128], qrT[:, qb, :], False, True)],
                            ["ckvnT", "krT", "qabsT%d" % qb, "qrT%d" % qb], ["psS%d%s" % (p % 2, "ab"[u])])
                    ACT(PT[:, p % 2, :], psS[p % 2][:, :], AF.Exp, ["psS%da" % (p % 2), "psS%db" % (p % 2)],
                        ["PT%da" % (p % 2), "PT%db" % (p % 2)], scale=SC_A)

                def pv(p):
                    mmo, mmd = [], []
                    for u in range(2):
                        kt = 2 * p + u
                        pt = PT[:, p % 2, u * 512:(u + 1) * 512]
                        mmo.append((psO[:, :], ckvn[:, kt, :], pt, kt == 0, kt == NT - 1))
                        mmd.append((psD[:, :], onesb[:, :], pt, kt == 0, kt == NT - 1))
                    MMG(mmo + mmd, ["PT%da" % (p % 2), "PT%db" % (p % 2), "ckvn", "onesb"], ["psO", "psD"])

                def fin_a(h=h):
                    ACT(TF[:, 3, :], psD[:, :], AF.Ln, ["psD"], [tfr(3)])
                    CP(TF[:, 4, :], psO[:, :], ["psO"], [tfr(4)])

                def fin_b(h=h):
                    if h % 4 == 0:
                        rg, rn, _ = w_next("win")
                        gA[h // 4] = (rg[:, 0:4096].rearrange("p (k n) -> p k n", n=512), rn)
                        w_prefetch()
                    rec = TF[:, 3, :]
                    ACT(rec, rec, AF.Exp, [tfr(3)], [tfr(3)], scale=-1.0)
                    ol = TF[:, 2, :].bitcast(BF16)[:, 0:512]
                    TT(ol, TF[:, 4, :], rec, ALU.mult, [tfr(4), tfr(3)], [tfr(2)])
                    MMG([(psM[:, :], Wvb[:, h, :], ol, True, True)], ["Wvb", tfr(2)], ["psM"])
                    oa = TF[:, 4, :]
                    CP(oa, psM[:, :], ["psM"], [tfr(4)])
                    wv, rn = gA[h // 4]
                    sg = gate_sig(wv, rn, h % 4, 5)
                    TT(oa, oa, sg[0], ALU.mult, [tfr(4), sg[1]], [tfr(4)])
                    TT(mixT[:, h, :], mixT[:, h, :], oa, ALU.add, [MIX[h], tfr(4)], [MIX[h]])

                heads.append((qk, pv, fin_a, fin_b))

            prev = None
            for h in range(H_A):
                for p in range(NP):
                    heads[h][0](p)
                    if prev is not None:
                        ph, pp = prev
                        heads[ph][1](pp)
                        if pp == NP - 1:
                            heads[ph][2]()
                            pend["fin"] = heads[ph][3]
                    prev = (h, p)
                    if p == min(2, NP - 1) and h + 1 < H_A:
                        q_project(h + 1, (h + 1) % 2)
                    if p == min(8, NP - 1) and pend["fin"] is not None:
                        pend["fin"]()
                        pend["fin"] = None
            heads[H_A - 1][1](NP - 1)
            heads[H_A - 1][2]()
            if pend["fin"] is not None:
                pend["fin"]()
            heads[H_A - 1][3]()
            pend["fin"] = None

            wo = []
            for i in range(2):
                rg, rn, _ = w_next("wout")
                wo.append((rg[:, 0:4096].rearrange("p (k n) -> p k n", n=512), rn))
            for hf in range(2):
                for j in range(4):
                    xt = xbuf[:, xb_i, j, :]
                    xnm = "x%d_%d" % (xb_i, j)
                    bank = psS[j % 2][:, hf * 512:(hf + 1) * 512]
                    bn = "psS%d%s" % (j % 2, "ab"[hf])
                    MMG([(bank, mixT[:, dk, j * 128:(j + 1) * 128], wo[hf][0][:, dk, :], dk == 0, dk == 7) for dk in range(8)],
                        MIX + [wo[hf][1]], [bn])
                    TT(xt[:, hf * 512:(hf + 1) * 512], xt[:, hf * 512:(hf + 1) * 512], bank, ALU.add, [xnm, bn], [xnm])
            w_prefetch()
            items = []
            for j in range(4):
                xt = xbuf[:, xb_i, j, :]
                xnm = "x%d_%d" % (xb_i, j)
                items.append((xt, [xnm], h2T[:, xb_i, :, :], ["h2T%d" % xb_i], j * 128, None))
            norm_group(items, g2T, "colsA")
            if c + 1 < NCH:
                CP(HB[:, c + 1, :, 0:1], h2T[:, xb_i, :, 511:512], ["h2T%d" % xb_i], ["HB%d" % (c + 1)])
            if c >= 1:
                CP(HB[:, c - 1, :, 1:2], h2T[:, xb_i, :, 0:1], ["h2T%d" % xb_i], ["HB%d" % (c - 1)])

        def stage_F2(c):
            xb_i = c % 2
            h2 = h2T[:, xb_i, :, :]
            h2n = "h2T%d" % xb_i
            hbanks = [(psO, "psO"), (psD, "psD"), (psM, "psM"), (psTf, "psT")]
            for hf in range(2):
                fl = 0
                for i in range(6):
                    rg, rn, (_, f0, nf) = w_next("wup")
                    wv = rg.rearrange("p (k n) -> p k n", n=512)
                    w_prefetch()
                    for k in range(nf):
                        f = f0 + k
                        par = fl % 2
                        u_tiles = []
                        for which in range(2):
                            ui = f if which == 0 else NF + f
                            bank = psS[par][:, which * 512:(which + 1) * 512]
                            bn = "psS%d%s" % (par, "ab"[which])
                            lcol = which * 256 + k * 128
                            MMG([(bank, wv[:, dk, lcol:lcol + 128], h2[:, dk, :], dk == 0, dk == 7) for dk in range(8)],
                                [h2n, rn], [bn])
                            hb, hbn = hbanks[par * 2 + which]
                            hcol = hb[:, 0:2]
                            MMG([(hcol, wv[:, dk, lcol:lcol + 128], HB[:, c, dk, :], dk == 0, dk == 7) for dk in range(8)],
                                ["HB%d" % c, rn], [hbn])
                            ut = TF[:, par * 2 + which, :]
                            un = tfr(par * 2 + which)
                            w0 = colsB[:, ui:ui + 1]
                            w1 = colsB[:, 44 + ui:44 + ui + 1]
                            w2 = colsB[:, 88 + ui:88 + ui + 1]
                            bb = colsA[:, 19 + ui:19 + ui + 1]
                            ACT(ut, bank, AF.Identity, [bn, "colsA", "colsB"], [un], scale=w1, bias=bb)
                            STT(ut[:, 1:512], bank[:, 0:511], w0, ut[:, 1:512], ALU.mult, ALU.add, [bn, un, "colsB"], [un])
                            STT(ut[:, 0:1], hcol[:, 0:1], w0, ut[:, 0:1], ALU.mult, ALU.add, [hbn, un, "colsB"], [un])
                            STT(ut[:, 0:511], bank[:, 1:512], w2, ut[:, 0:511], ALU.mult, ALU.add, [bn, un, "colsB"], [un])
                            STT(ut[:, 511:512], hcol[:, 1:2], w2, ut[:, 511:512], ALU.mult, ALU.add, [hbn, un, "colsB"], [un])
                            u_tiles.append((ut, un))
                        (ug, ugn), (uv, uvn) = u_tiles
                        et = TF[:, 4 + par, :]
                        en = tfr(4 + par)
                        ACT(et, ug, AF.Silu, [ugn], [en])
                        TT(B16[:, fl, :], et, uv, ALU.mult, [en, uvn], ["b16_%d" % fl])
                        fl += 1
                dbanks = [(psS[0][:, 0:512], "psS0a"), (psS[0][:, 512:1024], "psS0b"),
                          (psS[1][:, 0:512], "psS1a"), (psS[1][:, 512:1024], "psS1b"),
                          (psO[:, :], "psO"), (psD[:, :], "psD"), (psM[:, :], "psM"), (psTf[:, :], "psT")]
                for i in range(3):
                    rg, rn, (_, f0, nf) = w_next("wdown")
                    wv = rg.rearrange("p (f n) -> p f n", n=1024)
                    w_prefetch()
                    fl0 = f0 - hf * 11
                    for j in range(4):
                        xt = xbuf[:, xb_i, j, :]
                        xnm = "x%d_%d" % (xb_i, j)
                        for hh in range(2):
                            bank, bn = dbanks[2 * j + hh]
                            MMG([(bank, B16[:, fl0 + k, j * 128:(j + 1) * 128], wv[:, k, hh * 512:(hh + 1) * 512],
                                  i == 0 and k == 0, i == 2 and k == nf - 1) for k in range(nf)],
                                ["b16_%d" % (fl0 + k) for k in range(nf)] + [rn], [bn])
                            if i == 2:
                                TT(xt[:, hh * 512:(hh + 1) * 512], xt[:, hh * 512:(hh + 1) * 512], bank, ALU.add,
                                   [xnm, bn], [xnm])
                        if hf == 1 and i == 2:
                            t = 4 * c + j
                            oi = 7 if j % 2 == 0 else 4
                            ot = TF[:, oi:oi + 2, :].rearrange("p a b -> p (a b)")
                            otn = [tfr(oi), tfr(oi + 1)]
                            jk, jkn = xns[j % 2]
                            cc, nm = rms_rstd(xt, D, 2 if j % 2 == 0 else 5, [xnm], jk[:], [jkn])
                            STT(ot, xt, cc, gf_b[:], ALU.mult, ALU.mult, [xnm, nm, "gf_b"], otn)
                            DMA("sp", "oout%d" % (j % 2), [(out_d[t * 128:(t + 1) * 128, :], ot)], otn, [], is_output=True)

        stage_M(0)
        for c in range(NCH):
            if c + 1 < NCH:
                stage_M(c + 1)
            stage_F2(c)

        S.final_wait("sp")
        S.run(block)
    return nc


_CACHE = {}


def kernel(**inputs):
    x = np.ascontiguousarray(inputs["x"], dtype=np.float32)
    B, S_LEN, _ = x.shape
    NCH = S_LEN // 512
    if NCH not in _CACHE:
        _CACHE[NCH] = build(NCH)
    nc = _CACHE[NCH]
    consts = host_consts()
    shared = {}
    for k, v in inputs.items():
        if k == "x":
            continue
        a = np.ascontiguousarray(v)
        if k == "positions":
            a = a.astype(np.int32)
        else:
            a = a.astype(np.float32)
        shared[k] = a
    shared.update(consts)
    in_maps = []
    for b in range(B):
        m = dict(shared)
        m["x"] = x[b]
        in_maps.append(m)
    res = run_bass_kernel_spmd(nc, in_maps, core_ids=list(range(B)))
    out = np.stack([np.asarray(r["out"]) for r in res.results], axis=0)
    return out.astype(np.float32)
```
